# Optimizing a Trainium2 kernel written in Bass

```python
import jax, jax.numpy as jnp
from jax import lax
import numpy as np

D_MODEL = 1024
BATCH = 8
SEQ = 4096
DEPTH = 4

GRID_W = 64
CTX_LEN = 256
HEAD_DIM = 64
ROPE_THETA = 10000.0
NORM_EPS = 1e-6
NEG_INF = -1e30

GLA_HEADS = 4
GLA_DK = 32
GLA_DV = 64
GLA_RANK = 16
GLA_NORMALIZER = 16.0
GLA_CHUNK = 64
GLA_W = GLA_HEADS * GLA_DV

SWA_HEADS = 8
SWA_KV_HEADS = 2
SWA_GROUP = SWA_HEADS // SWA_KV_HEADS
SWA_WINDOW = 128
SWA_BLOCK = 128
SWA_W = SWA_HEADS * HEAD_DIM

RWKV_HEADS = 4
RWKV_N = 64
RWKV_W = RWKV_HEADS * RWKV_N
RWKV_W_RANK = 64
RWKV_A_RANK = 64
RWKV_G_RANK = 128
RWKV_GN_EPS = 64e-5

MIX_W = GLA_W + SWA_W + RWKV_W
D_FF = -(-8 * D_MODEL // (3 * 256)) * 256

GLA_SIZES = (GLA_HEADS * GLA_DK, GLA_HEADS * GLA_DK, GLA_W, GLA_W, GLA_RANK, GLA_RANK)
SWA_SIZES = (SWA_W, SWA_KV_HEADS * HEAD_DIM, SWA_KV_HEADS * HEAD_DIM)
RWKV_SIZES = (RWKV_W, RWKV_W, RWKV_W, RWKV_W_RANK, RWKV_W_RANK, RWKV_A_RANK, RWKV_A_RANK, RWKV_G_RANK)
GLA_COLS = sum(GLA_SIZES)
SWA_COLS = sum(SWA_SIZES)
RWKV_COLS = sum(RWKV_SIZES)
IN_COLS = GLA_COLS + SWA_COLS + RWKV_COLS

kernel_name = 'hybrid_gla_swa_rwkv7_dit_block'


def _split(p, sizes):
    return jnp.split(p, np.cumsum(sizes)[:-1].tolist(), axis=-1)


def rmsnorm(x, g):
    xf = x.astype(jnp.float32)
    y = xf * lax.rsqrt(jnp.mean(xf * xf, axis=-1, keepdims=True) + NORM_EPS)
    return (y * g.astype(jnp.float32)).astype(x.dtype)


def modulate(h, shift, scale):
    return h * (1.0 + scale) + shift


def axial_rope(n_tokens):
    rows = n_tokens // GRID_W
    row = jnp.repeat(jnp.arange(rows, dtype=jnp.float32), GRID_W)
    col = jnp.tile(jnp.arange(GRID_W, dtype=jnp.float32), rows)
    n_freq = HEAD_DIM // 4
    inv = ROPE_THETA ** (-jnp.arange(n_freq, dtype=jnp.float32) / n_freq)
    ang = jnp.concatenate([row[:, None] * inv, col[:, None] * inv], axis=-1)
    return jnp.cos(ang), jnp.sin(ang)


def apply_rope(x, cos, sin):
    shape = (cos.shape[0],) + (1,) * (x.ndim - 3) + (cos.shape[-1],)
    cos, sin = cos.reshape(shape), sin.reshape(shape)
    x1, x2 = jnp.split(x, 2, axis=-1)
    return jnp.concatenate([x1 * cos - x2 * sin, x1 * sin + x2 * cos], axis=-1)


def run_direction(scan_fn, ctx_in, lat_in, state0, reverse):
    if reverse:
        ctx_in = tuple(jnp.flip(t, axis=1) for t in ctx_in)
        lat_in = tuple(jnp.flip(t, axis=1) for t in lat_in)
    o_c, s_c = scan_fn(*ctx_in, state0)
    o_x, _ = scan_fn(*lat_in, s_c)
    if reverse:
        o_c, o_x = jnp.flip(o_c, axis=1), jnp.flip(o_x, axis=1)
    return o_c, o_x


def gla_chunked(q, k, v, glog, state0):
    B, T, H, _ = q.shape
    n = T // GLA_CHUNK
    rs = lambda a: a.reshape(B, n, GLA_CHUNK, H, a.shape[-1]).transpose(1, 0, 3, 2, 4)
    q, k, v, glog = rs(q), rs(k), rs(v), rs(glog)
    b = jnp.cumsum(glog, axis=-2)
    b_last = b[..., -1:, :]
    q_dec = q * jnp.exp(b)
    k_dec = k * jnp.exp(-b)
    k_to_end = k * jnp.exp(b_last - b)
    causal_in_chunk = jnp.tril(jnp.ones((GLA_CHUNK, GLA_CHUNK), jnp.float32))
    att = jnp.einsum('nbhid,nbhjd->nbhij', q_dec, k_dec) * causal_in_chunk
    o_intra = jnp.einsum('nbhij,nbhjv->nbhiv', att, v)

    def step(S, inp):
        qd, kte, vv, bl = inp
        o = jnp.einsum('bhid,bhdv->bhiv', qd, S)
        S = S * jnp.exp(bl)[:, :, 0, :, None] + jnp.einsum('bhjd,bhjv->bhdv', kte, vv)
        return S, o

    S_final, o_inter = lax.scan(step, state0, (q_dec, k_to_end, v, b_last))
    o = (o_intra + o_inter).transpose(1, 0, 3, 2, 4).reshape(B, T, H, v.shape[-1])
    return o, S_final


def gla_group(pc, px, decay_w, decay_b, norm_g, with_ctx_out):
    def prep(p):
        B, T, _ = p.shape
        q, k, v, g, lf, lb = _split(p, GLA_SIZES)
        heads = lambda t, d: t.reshape(B, T, GLA_HEADS, d)
        glog = lambda lr, i: heads(jax.nn.log_sigmoid(lr @ decay_w[i] + decay_b[i]) / GLA_NORMALIZER, GLA_DK)
        return (heads(q, GLA_DK) * GLA_DK ** -0.5, heads(k, GLA_DK), heads(v, GLA_DV), g,
                glog(lf, 0), glog(lb, 1))

    qc, kc, vc, gc, fc, bc = prep(pc)
    qx, kx, vx, gx, fx, bx = prep(px)
    zero = jnp.zeros((pc.shape[0], GLA_HEADS, GLA_DK, GLA_DV), jnp.float32)
    oc_f, ox_f = run_direction(gla_chunked, (qc, kc, vc, fc), (qx, kx, vx, fx), zero, False)
    oc_b, ox_b = run_direction(gla_chunked, (qc, kc, vc, bc), (qx, kx, vx, bx), zero, True)

    def out(o, g):
        B, T = g.shape[:2]
        return rmsnorm(o, norm_g).reshape(B, T, GLA_W) * jax.nn.silu(g)

    oc = out(oc_f + oc_b, gc) if with_ctx_out else None
    return oc, out(ox_f + ox_b, gx)


def swa_group(pc, px, sink, cos, sin, with_ctx_out):
    def prep(p):
        B, T, _ = p.shape
        q, k, v = _split(p, SWA_SIZES)
        return (q.reshape(B, T, SWA_KV_HEADS, SWA_GROUP, HEAD_DIM) * HEAD_DIM ** -0.5,
                k.reshape(B, T, SWA_KV_HEADS, HEAD_DIM), v.reshape(B, T, SWA_KV_HEADS, HEAD_DIM))

    qc, kc, vc = prep(pc)
    qx, kx, vx = prep(px)
    qx, kx = apply_rope(qx, cos, sin), apply_rope(kx, cos, sin)
    B, T = px.shape[:2]
    Tc = pc.shape[1]
    nb = T // SWA_BLOCK
    sink_logit = sink.astype(jnp.float32).reshape(SWA_KV_HEADS, SWA_GROUP)

    def bands(t):
        tp = jnp.pad(t, ((0, 0), (SWA_BLOCK, SWA_BLOCK), (0, 0), (0, 0)))
        tp = tp.reshape(B, nb + 2, SWA_BLOCK, SWA_KV_HEADS, HEAD_DIM)
        return jnp.concatenate([tp[:, :-2], tp[:, 1:-1], tp[:, 2:]], axis=2)

    kb, vb = bands(kx), bands(vx)
    rel = jnp.arange(SWA_BLOCK)[:, None] - jnp.arange(3 * SWA_BLOCK)[None, :] + SWA_BLOCK
    kpos = jnp.arange(nb)[:, None] * SWA_BLOCK - SWA_BLOCK + jnp.arange(3 * SWA_BLOCK)[None, :]
    mask = (jnp.abs(rel) <= SWA_WINDOW)[None] & ((kpos >= 0) & (kpos < T))[:, None, :]
    qb = qx.reshape(B, nb, SWA_BLOCK, SWA_KV_HEADS, SWA_GROUP, HEAD_DIM)

    def sink_col(s):
        return jnp.broadcast_to(sink_logit[None, :, :, None, None], s.shape[:-1] + (1,))

    def block(args):
        q, k, v, m = args
        s_ctx = jnp.einsum('bqkgd,bckd->bkgqc', q, kc)
        s_loc = jnp.where(m, jnp.einsum('bqkgd,bjkd->bkgqj', q, k), NEG_INF)
        p = jax.nn.softmax(jnp.concatenate([s_ctx, s_loc, sink_col(s_ctx)], axis=-1), axis=-1)
        return (jnp.einsum('bkgqc,bckd->bqkgd', p[..., :Tc], vc)
                + jnp.einsum('bkgqj,bjkd->bqkgd', p[..., Tc:Tc + 3 * SWA_BLOCK], v))

    ob = lax.map(block, (jnp.moveaxis(qb, 1, 0), jnp.moveaxis(kb, 1, 0), jnp.moveaxis(vb, 1, 0), mask))
    ox = jnp.moveaxis(ob, 0, 1).reshape(B, T, SWA_W)
    oc = None
    if with_ctx_out:
        s = jnp.einsum('bqkgd,bckd->bkgqc', qc, kc)
        p = jax.nn.softmax(jnp.concatenate([s, sink_col(s)], axis=-1), axis=-1)
        oc = jnp.einsum('bkgqc,bckd->bqkgd', p[..., :Tc], vc).reshape(B, Tc, SWA_W)
    return oc, ox


def centred_shift(p, w):
    pp = jnp.pad(p, ((0, 0), (1, 1), (0, 0)))
    return w[0] * pp[:, :-2] + w[1] * pp[:, 1:-1] + w[2] * pp[:, 2:]


def rwkv7_scan(r, w, k, v, kk, a, state0):
    xs = tuple(jnp.moveaxis(t, 1, 0) for t in (r, w, k, v, kk, a))

    def step(S, inp):
        rt, wt, kt, vt, kkt, at = inp
        sa = jnp.einsum('bhvk,bhk->bhv', S, kkt)
        S = (S * wt[:, :, None, :] - sa[..., None] * (kkt * at)[:, :, None, :]
             + vt[..., None] * kt[:, :, None, :])
        return S, jnp.einsum('bhvk,bhk->bhv', S, rt)

    S_final, o = lax.scan(step, state0, xs)
    return jnp.moveaxis(o, 0, 1), S_final


def rwkv_group(pc, px, shift_w, w0, w2, a0, a2, g2, k_k, k_a, r_k, ln_g, ln_b, with_ctx_out):
    def prep(p):
        B, T, _ = p.shape
        r, k, v, lwf, lwb, laf, lab, lg = _split(centred_shift(p, shift_w), RWKV_SIZES)
        heads = lambda t: t.reshape(B, T, RWKV_HEADS, RWKV_N)
        kk = heads(k * k_k)
        kk = kk * lax.rsqrt(jnp.sum(kk * kk, axis=-1, keepdims=True) + 1e-12)

        def direction(lw, la, i):
            z = w0[i] + jnp.tanh(lw) @ w2[i]
            decay = jnp.exp(-jnp.exp(-jax.nn.softplus(-z) - 0.5))
            a = jax.nn.sigmoid(a0[i] + la @ a2[i])
            return heads(decay), heads(k * (1.0 + (a - 1.0) * k_a)), heads(a)

        return (heads(r), heads(k), heads(v), kk, jax.nn.sigmoid(lg) @ g2,
                direction(lwf, laf, 0), direction(lwb, lab, 1))

    rc, kc, vc, kkc, gc, fc, bc = prep(pc)
    rx, kx, vx, kkx, gx, fx, bx = prep(px)
    zero = jnp.zeros((pc.shape[0], RWKV_HEADS, RWKV_N, RWKV_N), jnp.float32)

    def args(r, v, kk, d):
        decay, kd, a = d
        return (r, decay, kd, v, kk, a)

    oc_f, ox_f = run_direction(rwkv7_scan, args(rc, vc, kkc, fc), args(rx, vx, kkx, fx), zero, False)
    oc_b, ox_b = run_direction(rwkv7_scan, args(rc, vc, kkc, bc), args(rx, vx, kkx, bx), zero, True)
    gn_g = ln_g.reshape(RWKV_HEADS, RWKV_N)
    gn_b = ln_b.reshape(RWKV_HEADS, RWKV_N)

    def out(o, r, k, v, g):
        mu = jnp.mean(o, axis=-1, keepdims=True)
        var = jnp.mean(jnp.square(o - mu), axis=-1, keepdims=True)
        o = (o - mu) * lax.rsqrt(var + RWKV_GN_EPS) * gn_g + gn_b
        o = o + jnp.sum(r * k * r_k, axis=-1, keepdims=True) * v
        B, T = o.shape[:2]
        return o.reshape(B, T, RWKV_W) * g

    oc = out(oc_f + oc_b, rc, kc, vc, gc) if with_ctx_out else None
    return oc, out(ox_f + ox_b, rx, kx, vx, gx)


def swiglu(h, w_gate, w_up, w_down):
    return (jax.nn.silu(h @ w_gate) * (h @ w_up)) @ w_down


def setup_inputs(seed: int = 0) -> dict:
    key = jax.random.key(seed)
    ks = iter(jax.random.split(key, 40))
    nrm = lambda shape, scale: scale * jax.random.normal(next(ks), shape, jnp.float32)
    L, D = DEPTH, D_MODEL
    shift_base = jnp.array([0.25, 0.5, 0.25], jnp.float32)[None, :, None]
    return {
        'x': nrm((BATCH, SEQ, D), 1.0),
        'c': nrm((BATCH, D), 1.0),
        'ctx': nrm((BATCH, CTX_LEN, D), 1.0),
        'c_ctx': nrm((D,), 1.0),
        'ada_w': nrm((L, D, 6 * D), 0.5 * D ** -0.5),
        'ada_b': nrm((L, 6 * D), 0.02),
        'norm_mix_g': 1.0 + nrm((L, D), 0.05),
        'norm_ffn_g': 1.0 + nrm((L, D), 0.05),
        'w_in': nrm((L, D, IN_COLS), D ** -0.5),
        'w_out': nrm((L, MIX_W, D), MIX_W ** -0.5),
        'gla_decay_w': nrm((L, 2, GLA_RANK, GLA_HEADS * GLA_DK), GLA_RANK ** -0.5),
        'gla_decay_b': nrm((L, 2, GLA_HEADS * GLA_DK), 0.1),
        'gla_norm_g': 1.0 + nrm((L, GLA_DV), 0.05),
        'swa_sink': nrm((L, SWA_HEADS), 0.5),
        'rwkv_shift_w': shift_base + nrm((L, 3, RWKV_COLS), 0.05),
        'rwkv_w0': -1.0 + nrm((L, 2, RWKV_W), 0.5),
        'rwkv_w2': nrm((L, 2, RWKV_W_RANK, RWKV_W), RWKV_W_RANK ** -0.5),
        'rwkv_a0': nrm((L, 2, RWKV_W), 0.1),
        'rwkv_a2': nrm((L, 2, RWKV_A_RANK, RWKV_W), RWKV_A_RANK ** -0.5),
        'rwkv_g2': nrm((L, RWKV_G_RANK, RWKV_W), RWKV_G_RANK ** -0.5),
        'rwkv_k_k': 0.85 + nrm((L, RWKV_W), 0.05),
        'rwkv_k_a': 1.0 + nrm((L, RWKV_W), 0.05),
        'rwkv_r_k': nrm((L, RWKV_HEADS, RWKV_N), 0.1),
        'rwkv_ln_g': 1.0 + nrm((L, RWKV_W), 0.05),
        'rwkv_ln_b': nrm((L, RWKV_W), 0.02),
        'ffn_w_gate': nrm((L, D, D_FF), D ** -0.5),
        'ffn_w_up': nrm((L, D, D_FF), D ** -0.5),
        'ffn_w_down': nrm((L, D_FF, D), D_FF ** -0.5),
        'final_norm_g': 1.0 + nrm((D,), 0.05),
    }


def reference(x, c, ctx, c_ctx, ada_w, ada_b, norm_mix_g, norm_ffn_g, w_in, w_out,
              gla_decay_w, gla_decay_b, gla_norm_g, swa_sink,
              rwkv_shift_w, rwkv_w0, rwkv_w2, rwkv_a0, rwkv_a2, rwkv_g2, rwkv_k_k, rwkv_k_a,
              rwkv_r_k, rwkv_ln_g, rwkv_ln_b, ffn_w_gate, ffn_w_up, ffn_w_down, final_norm_g):
    T = x.shape[1]
    cos, sin = axial_rope(T)
    xc, xx = ctx, x
    for l in range(DEPTH):
        with_ctx_out = l < DEPTH - 1
        mod_x = (jax.nn.silu(c) @ ada_w[l] + ada_b[l])[:, None, :]
        mod_c = (jax.nn.silu(c_ctx) @ ada_w[l] + ada_b[l])[None, None, :]
        sh1x, sc1x, g1x, sh2x, sc2x, g2x = jnp.split(mod_x, 6, axis=-1)
        sh1c, sc1c, g1c, sh2c, sc2c, g2c = jnp.split(mod_c, 6, axis=-1)

        hx = modulate(rmsnorm(xx, norm_mix_g[l]), sh1x, sc1x)
        hc = modulate(rmsnorm(xc, norm_mix_g[l]), sh1c, sc1c)
        px = (hx @ w_in[l]).astype(jnp.float32)
        pc = (hc @ w_in[l]).astype(jnp.float32)
        px_a, px_b, px_c = _split(px, (GLA_COLS, SWA_COLS, RWKV_COLS))
        pc_a, pc_b, pc_c = _split(pc, (GLA_COLS, SWA_COLS, RWKV_COLS))
        oa_c, oa_x = gla_group(pc_a, px_a, gla_decay_w[l], gla_decay_b[l], gla_norm_g[l], with_ctx_out)
        ob_c, ob_x = swa_group(pc_b, px_b, swa_sink[l], cos, sin, with_ctx_out)
        oc_c, oc_x = rwkv_group(pc_c, px_c, rwkv_shift_w[l], rwkv_w0[l], rwkv_w2[l], rwkv_a0[l],
                                rwkv_a2[l], rwkv_g2[l], rwkv_k_k[l], rwkv_k_a[l], rwkv_r_k[l],
                                rwkv_ln_g[l], rwkv_ln_b[l], with_ctx_out)
        mix_x = jnp.concatenate([oa_x, ob_x, oc_x], axis=-1).astype(x.dtype) @ w_out[l]
        xx = xx + g1x * mix_x
        hx = modulate(rmsnorm(xx, norm_ffn_g[l]), sh2x, sc2x)
        xx = xx + g2x * swiglu(hx, ffn_w_gate[l], ffn_w_up[l], ffn_w_down[l])

        if with_ctx_out:
            mix_c = jnp.concatenate([oa_c, ob_c, oc_c], axis=-1).astype(x.dtype) @ w_out[l]
            xc = xc + g1c * mix_c
            hc = modulate(rmsnorm(xc, norm_ffn_g[l]), sh2c, sc2c)
            xc = xc + g2c * swiglu(hc, ffn_w_gate[l], ffn_w_up[l], ffn_w_down[l])
    return rmsnorm(xx, final_norm_g)
```

```python
import concourse.bass as bass
import concourse.mybir as mybir


class _Ins:
    __slots__ = ("eng", "fn", "dma", "deps", "signal", "sig", "sem", "val", "tag")

    def __init__(self, eng, fn, dma):
        self.eng = eng
        self.fn = fn
        self.dma = dma
        self.deps = ()
        self.signal = False
        self.sig = 0
        self.sem = None
        self.val = 0
        self.tag = None


class FW:
    NDMA = 8

    def __init__(self, nc):
        self.nc = nc
        self.engs = {"pe": nc.tensor, "dve": nc.vector, "act": nc.scalar,
                     "pool": nc.gpsimd, "sp": nc.sync}
        self.order = []
        self.state = {}
        self.last = {}
        self.pend = []

    def op(self, eng, fn, reads=(), writes=(), dma=False, tag=None):
        ins = _Ins(eng, fn, dma)
        ins.tag = tag
        if tag == "f32":
            ins.signal = True
        deps = {}
        st = self.state
        for k in reads:
            s = st.get(k)
            if s is not None and s[0] is not None:
                w = s[0]
                if w.dma or w.eng != eng or eng != "pe":
                    deps[id(w)] = w
        for k in writes:
            s = st.get(k)
            if s is not None:
                w = s[0]
                if w is not None and (w.dma or w.eng != eng or eng != "pe"):
                    deps[id(w)] = w
                for r in s[1].values():
                    if r.dma or r.eng != eng:
                        deps[id(r)] = r
                for r in s[2]:
                    deps[id(r)] = r
        ins.deps = tuple(deps.values())
        for d in ins.deps:
            d.signal = True
        for k in reads:
            s = st.get(k)
            if s is None:
                s = st[k] = [None, {}, []]
            if dma:
                s[2].append(ins)
            else:
                s[1][eng] = ins
        for k in writes:
            st[k] = [ins, {}, []]
        self.order.append(ins)
        if dma:
            self.pend.append(ins)
        else:
            self.last[eng] = ins
        return ins

    def barrier(self):
        deps = list(self.last.values()) + self.pend
        for d in deps:
            d.signal = True
        for e in self.engs:
            b = _Ins(e, None, False)
            b.deps = tuple(deps)
            self.order.append(b)
        self.pend = []
        self.last = {}
        self.state = {}

    def emit(self, final_waits=()):
        nc = self.nc
        import contextlib
        with contextlib.ExitStack() as es:
            esem = {e: es.enter_context(nc.semaphore("s_" + e)) for e in self.engs}
            dsem = {e: [es.enter_context(nc.semaphore("d_%s%d" % (e, i))) for i in range(self.NDMA)]
                    for e in self.engs}
            cnt = {e: 0 for e in self.engs}
            dcnt = {e: 0 for e in self.engs}
            dslot = {e: [None] * self.NDMA for e in self.engs}
            waited = {e: {} for e in self.engs}
            last_f32 = [None]

            def wait(eng, sem, val):
                w = waited[eng]
                k = id(sem)
                if w.get(k, 0) >= val:
                    return
                w[k] = val
                self.engs[eng].wait_ge(sem, val)

            for ins in self.order:
                e = ins.eng
                if ins.dma:
                    i = dcnt[e]
                    dcnt[e] += 1
                    slot = i % self.NDMA
                    prev = dslot[e][slot]
                    if prev is not None:
                        wait(e, prev.sem, prev.val)
                    ins.sem = dsem[e][slot]
                    ins.val = 16 * (i // self.NDMA + 1)
                    dslot[e][slot] = ins
                elif ins.signal:
                    cnt[e] += 1
                    ins.sem = esem[e]
                    ins.val = cnt[e]
                for d in ins.deps:
                    wait(e, d.sem, d.val)
                if ins.fn is None:
                    continue
                if ins.tag == "fwl" and last_f32[0] is not None:
                    wait(e, last_f32[0].sem, last_f32[0].val)
                    last_f32[0] = None
                elif ins.tag == "f32":
                    last_f32[0] = ins
                bi = ins.fn(self.engs[e])
                if ins.dma:
                    bi.then_inc(ins.sem, 16)
                elif ins.signal:
                    bi.then_inc(ins.sem, 1)
                ins.fn = None
            for e in self.engs:
                for prev in dslot[e]:
                    if prev is not None:
                        wait(e, prev.sem, prev.val)
            self.counts = (cnt, dcnt)
import numpy as np
import contextlib
import concourse.bass as bass
import concourse.mybir as mybir
from concourse.bass_utils import run_bass_kernel_spmd

F32 = mybir.dt.float32
BF16 = mybir.dt.bfloat16
ALU = mybir.AluOpType
AF = mybir.ActivationFunctionType
AX = mybir.AxisListType

D = 1024
CTX = 256
NFM = 12
LFB0 = NFM * 128
TM0 = LFB0 + 32
NTM = 1920
WR = TM0 + NTM
DFF = 2816
NFC = 22
GLA_SCALE = 32 ** -0.5


def relayout_w_in(w_in):
    L = w_in.shape[0]
    g0, s0, r0 = 0, 800, 1568
    gq = w_in[:, :, 0:128]; gk = w_in[:, :, 128:256]; gv = w_in[:, :, 256:512]; gg = w_in[:, :, 512:768]
    lfb = w_in[:, :, 768:800]
    sq = w_in[:, :, s0:s0 + 512]; sk = w_in[:, :, s0 + 512:s0 + 640]; sv = w_in[:, :, s0 + 640:s0 + 768]
    rw = w_in[:, :, r0:r0 + 1152]

    def swap(a):
        sh = a.shape
        a = a.reshape(sh[0], sh[1], sh[2] // 64, 2, 32)
        return a[:, :, :, ::-1, :].reshape(sh)

    return np.ascontiguousarray(np.concatenate(
        [gq, gk, sq, swap(sq), sk, swap(sk), lfb, gk, gv, gg, sv, rw], axis=2))


def rope_tables(TL):
    rows = TL // 64
    row = np.repeat(np.arange(rows, dtype=np.float32), 64)
    col = np.tile(np.arange(64, dtype=np.float32), rows)
    inv = (10000.0 ** (-np.arange(16, dtype=np.float32) / 16)).astype(np.float32)
    ang = np.concatenate([row[:, None] * inv, col[:, None] * inv], axis=-1)
    cos = np.cos(ang).astype(np.float32).T
    sin = np.sin(ang).astype(np.float32).T
    cosT = np.concatenate([np.ones((32, CTX), np.float32), cos], axis=1)
    sinT = np.concatenate([np.zeros((32, CTX), np.float32), sin], axis=1)
    cos128 = np.concatenate([cosT, cosT, cosT, cosT], axis=0)
    sin128 = np.concatenate([-sinT, sinT, -sinT, sinT], axis=0)
    return np.ascontiguousarray(cos128), np.ascontiguousarray(sin128)


class K:
    pass


def build(TL, DEPTH, dbg=(), phases=('P', 'GLA', 'SWA', 'RWKV', 'FFN')):
    T = CTX + TL
    NT128 = T // 128
    nc = bass.Bass("TRN2", target_bir_lowering=False)
    fw = FW(nc)
    es = contextlib.ExitStack()

    def din(name, shape, dt=F32):
        return nc.dram_tensor(name, list(shape), dt, kind="ExternalInput").ap()

    def dscr(name, shape, dt=F32):
        kind = "ExternalOutput" if name in dbg else "Internal"
        return nc.dram_tensor(name, list(shape), dt, kind=kind).ap()

    L = DEPTH
    xin = din("xin", [T, D])
    cc = din("cc", [128, 8, 2])
    ada_w = din("ada_w", [L, D, 6 * D])
    ada_b = din("ada_b", [L, 6 * D])
    nmg = din("nmg", [L, 128, 8])
    nfg = din("nfg", [L, 128, 8])
    wr = din("wr", [L, D, WR])
    w_out = din("w_out", [L, D, D])
    gdw = din("gla_decay_w", [L, 2, 16, 128])
    gdb = din("gla_decay_b", [L, 2, 128])
    gng = din("gla_norm_g", [L, 64])
    sink = din("swa_sink", [L, 8])
    r_shw = din("rwkv_shift_w", [L, 3, 1152])
    r_w0 = din("rwkv_w0", [L, 2, 256])
    r_w2 = din("rwkv_w2", [L, 2, 64, 256])
    r_a0 = din("rwkv_a0", [L, 2, 256])
    r_a2 = din("rwkv_a2", [L, 2, 64, 256])
    r_g2 = din("rwkv_g2", [L, 128, 256])
    r_kk = din("rwkv_k_k", [L, 256])
    r_ka = din("rwkv_k_a", [L, 256])
    r_rk = din("rwkv_r_k", [L, 256])
    r_lng = din("rwkv_ln_g", [L, 256])
    r_lnb = din("rwkv_ln_b", [L, 256])
    wg = din("ffn_w_gate", [L, D, DFF])
    wu = din("ffn_w_up", [L, D, DFF])
    wd = din("ffn_w_down", [L, DFF, D])
    fng = din("final_norm_g", [D])
    cosd = din("cosT", [128, T])
    sind = din("sinT", [128, T])
    out = nc.dram_tensor("out", [TL, D], F32, kind="ExternalOutput").ap()

    xs = dscr("xs", [T, D])
    gates = dscr("gates", [L, 2, 2048])
    gq_d = dscr("gq", [128, T]); gk_d = dscr("gk", [128, T])
    gtm_d = dscr("gtm", [T, 640])
    glog_d = dscr("glog", [T, 256])
    sq_d = dscr("sq", [512, T], BF16); sk_d = dscr("sk", [128, T], BF16)
    sv_d = dscr("sv", [T, 128], BF16)
    rw_d = dscr("rw", [T, 1152])
    rws_d = dscr("rws", [T, 1152])
    mix_d = din("mixT", [D, T], BF16) if "mixT_in" in dbg else dscr("mixT", [D, T], BF16)
    wgb_d = dscr("wgb", [NFC, 128, 8 * 128], BF16)
    wub_d = dscr("wub", [NFC, 128, 8 * 128], BF16)
    wdb_d = dscr("wdb", [NFC, 128, D], BF16)

    ARENA_F32 = 52 * 1024 - 512
    arena_t = es.enter_context(nc.sbuf_tensor("arena", [128, ARENA_F32], F32))

    class Arena:
        def __init__(s, lo, hi):
            s.lo = lo; s.hi = hi; s.off = lo
        def reset(s):
            s.off = s.lo
        def f32(s, n, parts=128):
            v = arena_t[0:parts, s.off:s.off + n]
            s.off += n
            assert s.off <= s.hi, ("arena overflow", s.off, s.hi)
            return v
        def bf16(s, n, parts=128):
            m = (n + 1) // 2
            v = arena_t[0:parts, s.off:s.off + m].bitcast(BF16)
            s.off += m
            assert s.off <= s.hi, ("arena overflow", s.off, s.hi)
            return v

    PERS = 3072
    pers = Arena(0, PERS)
    ar = Arena(PERS, ARENA_F32)
    PS = [es.enter_context(nc.psum_tensor("ps%d" % i, [128, 512], F32))[:, :] for i in range(8)]

    op = fw.op

    def pe_tag(wap):
        if wap.dtype == F32:
            return "f32"
        return "fwl" if wap.shape[-1] >= 128 else None

    def mm(out, lhsT, rhs, r, w, start=True, stop=True, tp=None):
        kw = {"tile_position": tp} if tp is not None else {}
        op("pe", lambda e: e.matmul(out, lhsT=lhsT, rhs=rhs, start=start, stop=stop, **kw), reads=r, writes=w, tag=pe_tag(lhsT))

    def tr(out, in_, ident, r, w):
        op("pe", lambda e: e.transpose(out, in_, ident), reads=r, writes=w, tag=pe_tag(in_))

    def dma(eng, out, in_, r, w, **kw):
        op(eng, lambda e: e.dma_start(out=out, in_=in_, **kw), reads=r, writes=w, dma=True)

    def act(out, in_, func, r, w, bias=None, scale=None, accum=None):
        kw = {}
        if bias is not None: kw["bias"] = bias
        if scale is not None: kw["scale"] = scale
        if accum is not None: kw["accum_out"] = accum
        op("act", lambda e: e.activation(out=out, in_=in_, func=func, **kw), reads=r, writes=w)

    def tt(eng, out, in0, in1, aop, r, w):
        op(eng, lambda e: e.tensor_tensor(out=out, in0=in0, in1=in1, op=aop), reads=r, writes=w)

    def ts(eng, out, in0, s1, s2, op0, op1, r, w, accum=None):
        kw = {}
        if op1 is not None: kw["op1"] = op1
        if accum is not None: kw["accum_out"] = accum
        op(eng, lambda e: e.tensor_scalar(out=out, in0=in0, scalar1=s1, scalar2=s2, op0=op0, **kw), reads=r, writes=w)

    def stt(eng, out, in0, scalar, in1, op0, op1, r, w):
        op(eng, lambda e: e.scalar_tensor_tensor(out=out, in0=in0, scalar=scalar, in1=in1, op0=op0, op1=op1), reads=r, writes=w)

    def cp(eng, out, in_, r, w):
        if eng == "act":
            op("act", lambda e: e.activation(out=out, in_=in_, func=AF.Copy), reads=r, writes=w)
        else:
            op(eng, lambda e: e.tensor_copy(out=out, in_=in_), reads=r, writes=w)

    def memset(eng, ap, val, w):
        op(eng, lambda e: e.memset(ap, val), writes=w)

    def recip(out, in_, r, w):
        op("dve", lambda e: e.reciprocal(out=out, in_=in_), reads=r, writes=w)

    def barrier():
        fw.barrier()

    def tri_mask(t_, key, kind, n):
        cm, st, base = {"ge": (1, -1, 0), "le": (-1, 1, 0), "gt": (1, -1, -1), "lt": (-1, 1, -1)}[kind]
        memset("pool", t_, 1.0, [key])
        op("pool", lambda e: e.affine_select(out=t_, in_=t_, pattern=[[st, n]], compare_op=ALU.is_ge,
                                             fill=0.0, base=base, channel_multiplier=cm), reads=[key], writes=[key])

    identb = pers.bf16(128)
    identf = pers.f32(128)
    ones1 = pers.f32(128, parts=1)
    epsc = pers.f32(1)
    for nm, t_ in (("identb", identb), ("identf", identf)):
        memset("pool", t_, 1.0, [nm])
        op("pool", (lambda t_: (lambda e: e.affine_select(out=t_, in_=t_, pattern=[[-1, 128]], compare_op=ALU.is_equal,
                                                         fill=0.0, base=0, channel_multiplier=1)))(t_), reads=[nm], writes=[nm])
    memset("pool", ones1, 1.0, ["ones1"])
    memset("pool", epsc, 1e-6, ["epsc"])
    onec = pers.f32(1)
    memset("pool", onec, 1.0, ["onec"])
    A1 = pers.f32(L * 16).rearrange("p (l k t) -> p l k t", l=L, k=8)
    B1 = pers.f32(L * 16).rearrange("p (l k t) -> p l k t", l=L, k=8)
    A2 = pers.f32(L * 16).rearrange("p (l k t) -> p l k t", l=L, k=8)
    B2 = pers.f32(L * 16).rearrange("p (l k t) -> p l k t", l=L, k=8)

    def phase_ada():
        ar.reset()
        cs_raw = ar.f32(16).rearrange("p (k t) -> p k t", k=8)
        cs = ar.f32(16).rearrange("p (k t) -> p k t", k=8)
        dma("sp", cs_raw, cc, [], ["cs_raw"])
        act(cs, cs_raw, AF.Silu, ["cs_raw"], ["cs"])
        nmg_t = ar.f32(L * 8).rearrange("p (l k) -> p l k", l=L)
        nfg_t = ar.f32(L * 8).rearrange("p (l k) -> p l k", l=L)
        dma("sp", nmg_t, nmg.rearrange("l p k -> p l k"), [], ["nmg_t"])
        dma("sp", nfg_t, nfg.rearrange("l p k -> p l k"), [], ["nfg_t"])
        modfm = ar.f32(96).rearrange("p (j t) -> p j t", t=2)
        tmp = ar.f32(16).rearrange("p (k t) -> p k t", k=8)
        abrow = ar.f32(6 * D, parts=1)
        awt = [ar.f32(8 * 1024).rearrange("p (k c) -> p k c", k=8) for _ in range(2)]
        grow = ar.f32(2048, parts=2)
        PSa = PS[0][:, 0:96].rearrange("p (j t) -> p j t", t=2)
        for l in range(L):
            dma("sp", abrow, ada_b[l:l + 1, :], [], ["abrow"])
            for s in range(6):
                a = awt[s % 2]
                ak = ("awt", s % 2)
                for kh in range(2):
                    dma(("sp", "pool")[kh], a[:, 4 * kh:4 * kh + 4, :],
                        ada_w[l, 512 * kh:512 * (kh + 1), s * 1024:(s + 1) * 1024].rearrange("(k p) c -> p k c", p=128), [], [ak])
                for jj in range(8):
                    j = s * 8 + jj
                    for kc in range(8):
                        mm(PSa[:, j, :], a[:, kc, jj * 128:(jj + 1) * 128], cs[:, kc, :], [ak, "cs"], ["PSa"], start=(kc == 0), stop=False)
                    mm(PSa[:, j, :], abrow[0:1, j * 128:(j + 1) * 128], ones1[0:1, 0:2], ["abrow", "ones1"], ["PSa"], start=False, stop=True)
                if s in (2, 5):
                    for h in range(2):
                        pg = PS[1 + h][0:2, :]
                        pk = ("PSgt", h)
                        for kc in range(8):
                            mm(pg, cs[:, kc, :], a[:, kc, h * 512:(h + 1) * 512], [ak, "cs"], [pk], start=(kc == 0), stop=False)
                        mm(pg, ones1[0:1, 0:2], abrow[0:1, s * 1024 + h * 512: s * 1024 + (h + 1) * 512], ["abrow", "ones1"], [pk], start=False, stop=True)
                        o0 = (0 if s == 2 else 1024) + h * 512
                        cp("dve", grow[:, o0:o0 + 512], pg, [pk], ["grow"])
            dma("pool", gates[l], grow, ["grow"], [("gates", l)])
            cp("dve", modfm, PSa, ["PSa"], ["modfm"])
            for (Ax, Bx, gt, gk_, s_sc, s_sh) in ((A1, B1, nmg_t, "nmg_t", 1, 0), (A2, B2, nfg_t, "nfg_t", 4, 3)):
                ts("dve", tmp, modfm[:, s_sc * 8:(s_sc + 1) * 8, :], 1.0, None, ALU.add, None, ["modfm"], ["adatmp"])
                tt("dve", Ax[:, l], tmp, gt[:, l, :].unsqueeze(2).to_broadcast([128, 8, 2]), ALU.mult, ["adatmp", gk_], ["mod"])
                cp("dve", Bx[:, l], modfm[:, s_sh * 8:(s_sh + 1) * 8, :], ["modfm"], ["mod"])
        barrier()

    def norm_mod_T(xt, xk, A, B, l, ti, hT, hk, col0, scr):
        junk, ss, rstd, xn, PT = scr
        act(junk, xt, AF.Square, [xk], ["junk", "ss"], accum=ss)
        act(rstd, ss, AF.Sqrt, ["ss", "epsc"], ["rstd"], bias=epsc, scale=1.0 / D)
        recip(rstd, rstd, ["rstd"], ["rstd"])
        ts("dve", xn, xt, rstd, None, ALU.mult, None, [xk, "rstd"], ["xn"])
        for kc in range(8):
            tr(PT[:, kc * 128:(kc + 1) * 128], xn[:, kc * 128:(kc + 1) * 128], identb, ["xn", "identb"], ["PT"])
        for kc in range(8):
            act(hT[:, kc, col0:col0 + 128], PT[:, kc * 128:(kc + 1) * 128], AF.Identity, ["PT", "mod"], [hk],
                bias=B[:, l, kc, ti:ti + 1], scale=A[:, l, kc, ti:ti + 1])

    def tiles():
        res = [(0, CTX, 1)]
        t0 = CTX
        while t0 < T:
            res.append((t0, 512, 0))
            t0 += 512
        return res

    def phase_P(l):
        ar.reset()
        src = xin if l == 0 else xs
        wbf = ar.bf16(8 * WR).rearrange("p (k c) -> p k c", k=8)
        cosTs = [ar.f32(512) for _ in range(2)]; sinTs = [ar.f32(512) for _ in range(2)]
        ncs = 0
        wdaug = ar.f32(256, parts=64)
        memset("pool", wdaug, 0.0, ["wdaug"])
        for d_ in range(2):
            dma("sp", wdaug[32 * d_:32 * d_ + 16, 128 * d_:128 * (d_ + 1)], gdw[l, d_], [], ["wdaug"])
            dma("sp", wdaug[32 * d_ + 16:32 * d_ + 17, 128 * d_:128 * (d_ + 1)], gdb[l, d_:d_ + 1, :], [], ["wdaug"])
        lfbT = ar.f32(512, parts=64)
        memset("pool", lfbT, 1.0, ["lfbT"])
        stg = [ar.f32(WR) for _ in range(2)]
        for kc in range(8):
            sk_ = ("stg", kc % 2)
            dma(("sp", "pool")[kc % 2], stg[kc % 2], wr[l, kc * 128:(kc + 1) * 128, :], [], [sk_])
            cp(("dve", "act", "pool")[kc % 3], wbf[:, kc, :], stg[kc % 2], [sk_], [("wbf", kc)])
        wk = [("wbf", kc) for kc in range(8)]
        hT = ar.bf16(8 * 512).rearrange("p (k c) -> p k c", k=8)
        xts = [ar.f32(D) for _ in range(3)]
        junk = ar.bf16(D); ss = ar.f32(1); rstd = ar.f32(1); xn = ar.bf16(D)
        PT = PS[7].bitcast(BF16)
        scr = (junk, ss, rstd, xn, PT)
        fmo = [ar.f32(512) for _ in range(2)]
        t1s = [ar.f32(512) for _ in range(2)]; t2s = [ar.f32(512) for _ in range(2)]
        rob = [ar.bf16(512) for _ in range(2)]
        tmrow = [ar.f32(NTM) for _ in range(2)]
        svb = [ar.bf16(128) for _ in range(2)]
        ge = [ar.f32(256) for _ in range(2)]
        nx = 0; nfm = 0; nro = 0; ntm = 0
        for (t0, NT, ti) in tiles():
            nsub = NT // 128
            cosT = cosTs[ncs % 2]; sinT = sinTs[ncs % 2]; csk = ("cosT", ncs % 2); snk = ("sinT", ncs % 2); ncs += 1
            dma("sp", cosT[:, :NT], cosd[:, t0:t0 + NT], [], [csk]); dma("sp", sinT[:, :NT], sind[:, t0:t0 + NT], [], [snk])
            for j in range(nsub):
                xt = xts[nx % 3]; xk = ("xt", nx % 3); nx += 1
                dma("sp", xt, src[t0 + 128 * j:t0 + 128 * (j + 1), :], [], [xk])
                norm_mod_T(xt, xk, A1, B1, l, ti, hT, ("hT", j), 128 * j, scr)
            hk = [("hT", j) for j in range(nsub)]
            import os
            PSTOP = int(os.environ.get("PSTOP", "9"))
            if PSTOP <= 1:
                continue
            def fm_mm(c, ps, pk):
                for kc in range(8):
                    mm(ps[:, :NT], wbf[:, kc, c * 128:(c + 1) * 128], hT[:, kc, :NT], [wk[kc]] + hk, [pk], start=(kc == 0), stop=(kc == 7))
            for c, dst in ((0, gq_d), (1, gk_d)):
                ps = PS[c]; pk = ("PS", c)
                fm_mm(c, ps, pk)
                o = fmo[nfm % 2]; ok_ = ("fmo", nfm % 2); nfm += 1
                cp("act", o[:, :NT], ps[:, :NT], [pk], [ok_])
                dma("pool", dst[:, t0:t0 + NT], o[:, :NT], [ok_], [])
            for idx, (c, c2, dst, r0, scl) in enumerate([(2, 6, sq_d, 0, 0.125), (3, 7, sq_d, 128, 0.125), (4, 8, sq_d, 256, 0.125),
                                                         (5, 9, sq_d, 384, 0.125), (10, 11, sk_d, 0, 1.0)]):
                pa = PS[(2 * idx) % 4]; pka = ("PS", (2 * idx) % 4)
                pb = PS[(2 * idx + 1) % 4]; pkb = ("PS", (2 * idx + 1) % 4)
                fm_mm(c, pa, pka); fm_mm(c2, pb, pkb)
                t1 = t1s[nro % 2]; t2 = t2s[nro % 2]; ob = rob[nro % 2]
                k1 = ("t1", nro % 2); k2 = ("t2", nro % 2); kob = ("rob", nro % 2); nro += 1
                stt("dve", t1[:, :NT], pa[:, :NT], scl, cosT[:, :NT], ALU.mult, ALU.mult, [pka, csk], [k1])
                stt("dve", t2[:, :NT], pb[:, :NT], scl, sinT[:, :NT], ALU.mult, ALU.mult, [pkb, snk], [k2])
                tt("pool", ob[:, :NT], t1[:, :NT], t2[:, :NT], ALU.add, [k1, k2], [kob])
                dma("pool", dst[r0:r0 + 128, t0:t0 + NT], ob[:, :NT], [kob], [])
            if PSTOP <= 2:
                continue
            pl = PS[2]; pkl = ("PS", 2)
            for kc in range(8):
                mm(pl[0:16, :NT], wbf[:, kc, LFB0:LFB0 + 16], hT[:, kc, :NT], [wk[kc]] + hk, [pkl], start=(kc == 0), stop=(kc == 7))
            for kc in range(8):
                mm(pl[32:48, :NT], wbf[:, kc, LFB0 + 16:LFB0 + 32], hT[:, kc, :NT], [wk[kc]] + hk, [pkl], start=(kc == 0), stop=(kc == 7), tp=(0, 32))
            cp("act", lfbT[0:16, :NT], pl[0:16, :NT], [pkl], ["lfbT"])
            cp("act", lfbT[32:48, :NT], pl[32:48, :NT], [pkl], ["lfbT"])
            if PSTOP <= 3:
                continue
            for j in range(nsub):
                tm = tmrow[ntm % 2]; tk = ("tmrow", ntm % 2)
                sv_ = svb[ntm % 2]; svk = ("svb", ntm % 2)
                g_ = ge[ntm % 2]; gk_ = ("ge", ntm % 2); ntm += 1
                tok = slice(t0 + 128 * j, t0 + 128 * (j + 1))
                for gi, (o, n) in enumerate(((0, 512), (512, 512), (1024, 512), (1536, 384))):
                    ps = PS[4 + gi % 3]; pk = ("PS", 4 + gi % 3)
                    for kc in range(8):
                        mm(ps[:, :n], hT[:, kc, 128 * j:128 * (j + 1)], wbf[:, kc, TM0 + o:TM0 + o + n], [wk[kc], hk[j]], [pk],
                           start=(kc == 0), stop=(kc == 7))
                    cp(("act", "dve")[gi % 2], tm[:, o:o + n], ps[:, :n], [pk], [tk])
                cp("pool", sv_, tm[:, 640:768], [tk], [svk])
                dma("pool", gtm_d[tok, :], tm[:, 0:640], [tk], [])
                dma("pool", sv_d[tok, :], sv_, [svk], [])
                dma("pool", rw_d[tok, :], tm[:, 768:1920], [tk], [])
                if PSTOP <= 4:
                    continue
                pg = PS[4]; pk = ("PS", 4)
                mm(pg[:, 0:256], lfbT[0:64, 128 * j:128 * (j + 1)], wdaug[0:64, :], ["lfbT", "wdaug"], [pk])
                if PSTOP <= 5:
                    cp("act", g_, pg[:, 0:256], [pk], [gk_])
                else:
                    act(g_, pg[:, 0:256], AF.Exp, [pk], [gk_], scale=-1.0)
                    act(g_, g_, AF.Ln, [gk_, "onec"], [gk_], bias=onec)
                ts("dve", g_, g_, -1.0 / 16.0, None, ALU.mult, None, [gk_], [gk_])
                dma("pool", glog_d[tok, :], g_, [gk_], [])
        barrier()
    def phase_GLA(l):
        ar.reset()
        NCH = T // 64
        NCC = CTX // 64
        tri = {}
        for nm, kind in (("tf", "le"), ("tb", "ge"), ("rf", "gt"), ("rb", "lt")):
            t_ = ar.f32(64, parts=64)
            tri_mask(t_, "g_" + nm, kind, 64)
            tri[nm] = t_
        Eh = ar.f32(128, parts=4); Ev = ar.f32(256, parts=4)
        for t_, nm, w_, n_ in ((Eh, "Eh", 32, 128), (Ev, "Ev", 64, 256)):
            memset("pool", t_, 1.0, [nm])
            op("pool", (lambda t_, w_, n_: (lambda e: e.affine_select(out=t_, in_=t_, pattern=[[1, n_]], compare_op=ALU.is_ge,
                                                                      fill=0.0, base=0, channel_multiplier=-w_)))(t_, w_, n_), reads=[nm], writes=[nm])
            op("pool", (lambda t_, w_, n_: (lambda e: e.affine_select(out=t_, in_=t_, pattern=[[-1, n_]], compare_op=ALU.is_ge,
                                                                      fill=0.0, base=w_ - 1, channel_multiplier=w_)))(t_, w_, n_), reads=[nm], writes=[nm])
        bdm = ar.f32(256); hm = ar.f32(4)
        mm(PS[0][:, 0:256], Eh, Ev, ["Eh", "Ev"], [("PS", 0)])
        mm(PS[1][:, 0:4], Eh, identf[0:4, 0:4], ["Eh", "identf"], [("PS", 1)])
        cp("dve", bdm, PS[0][:, 0:256], [("PS", 0)], ["bdm"])
        cp("dve", hm, PS[1][:, 0:4], [("PS", 1)], ["hm"])
        gngb = ar.f32(256, parts=64)
        for h in range(4):
            dma("sp", gngb[:, 64 * h:64 * (h + 1)], gng[l:l + 1, :].partition_broadcast(64), [], ["gngb"])
        eps64 = ar.f32(1, parts=64)
        memset("pool", eps64, 1e-6, ["eps64"])
        obuf = ar.f32(NCH * 256, parts=64).rearrange("p (c n) -> p c n", c=NCH)
        dirs = []
        for d in range(2):
            o = K()
            o.d = d
            o.qTg = ar.f32(512); o.kTg = ar.f32(512)
            o.gtm = ar.f32(8 * 640, parts=64).rearrange("p (c n) -> p c n", c=8)
            o.glg = ar.f32(8 * 128, parts=64).rearrange("p (c n) -> p c n", c=8)
            o.eb = ar.f32(64); o.enb = ar.f32(64); o.qd = ar.f32(64); o.kd = ar.bf16(64); o.gam = ar.f32(1)
            o.kte = ar.f32(128, parts=64); o.kteb = ar.bf16(128, parts=64)
            o.vbg = ar.bf16(8 * 256, parts=64).rearrange("p (c n) -> p c n", c=8)
            o.qh = ar.bf16(256); o.attm = ar.bf16(256, parts=64); o.tkv = ar.f32(256)
            o.S = [ar.f32(256), ar.f32(256)]
            o.ns = 0
            o.grp = None
            memset("pool", o.S[0], 0.0, [("S", d, 0)])
            dirs.append(o)
        osum = ar.f32(256, parts=64); osq = ar.f32(256, parts=64); ss4 = ar.f32(4, parts=64); sil = ar.f32(256, parts=64)
        stgs = {}
        PB = {0: (PS[0], PS[1], PS[2], PS[3]), 1: (PS[4], PS[5], PS[6], PS[7])}

        def grp_of(c):
            return (0, 0, NCC) if c < NCC else (1 + (c - NCC) // 8, NCC + ((c - NCC) // 8) * 8, 8)

        def chunk(o, c, first_visit):
            d = o.d
            gi, c0, gn = grp_of(c)
            kq = ("qTg", d); kg = ("gtm", d); kl = ("glg", d)
            if o.grp != gi:
                o.grp = gi
                t0g = c0 * 64
                dma("sp", o.qTg[:, 0:gn * 64], gq_d[:, t0g:t0g + gn * 64], [], [kq])
                dma("sp", o.kTg[:, 0:gn * 64], gk_d[:, t0g:t0g + gn * 64], [], [kq])
                dma("sp", o.gtm[:, 0:gn, :], gtm_d[t0g:t0g + gn * 64, :].rearrange("(c p) n -> p c n", p=64), [], [kg])
                dma("sp", o.glg[:, 0:gn, :], glog_d[t0g:t0g + gn * 64, 128 * d:128 * (d + 1)].rearrange("(c p) n -> p c n", p=64), [], [kl])
                cp("pool", o.vbg[:, 0:gn, :], o.gtm[:, 0:gn, 128:384], [kg], [("vbg", d)])
            ci = c - c0
            qT = o.qTg[:, ci * 64:(ci + 1) * 64]; kT = o.kTg[:, ci * 64:(ci + 1) * 64]
            ktok = o.gtm[:, ci, 0:128]; v = o.vbg[:, ci, :]; kvk = ("vbg", d)
            glog = o.glg[:, ci, :]
            P0, P1, P2, P3 = PB[d]
            pk = [("PS", 4 * d + i) for i in range(4)]
            T_ = tri["tf" if d == 0 else "tb"]; R_ = tri["rf" if d == 0 else "rb"]
            tkn = "g_tf" if d == 0 else "g_tb"; rkn = "g_rf" if d == 0 else "g_rb"
            mm(P0[:, 0:64], glog, T_, [kl, tkn], [pk[0]])
            mm(P0[0:64, 64:192], R_, glog, [kl, rkn], [pk[0]])
            ke = ("gtmp", d)
            yield
            act(o.eb, P0[:, 0:64], AF.Exp, [pk[0]], [(ke, "eb")])
            act(o.enb, P0[:, 0:64], AF.Exp, [pk[0]], [(ke, "enb")], scale=-1.0)
            lastcol = 63 if d == 0 else 0
            act(o.gam, P0[:, lastcol:lastcol + 1], AF.Exp, [pk[0]], [(ke, "gam")])
            act(o.kte, P0[0:64, 64:192], AF.Exp, [pk[0]], [(ke, "kte")])
            yield
            tt("dve", o.kteb, o.kte, ktok, ALU.mult, [(ke, "kte"), kg], [(ke, "kteb")])
            stt("dve", o.qd, qT, GLA_SCALE, o.eb, ALU.mult, ALU.mult, [kq, (ke, "eb")], [(ke, "qd")])
            tt("dve", o.kd, kT, o.enb, ALU.mult, [kq, (ke, "enb")], [(ke, "kd")])
            tt("dve", o.qh.rearrange("p (h t) -> p h t", h=4), o.qd.unsqueeze(1).to_broadcast([128, 4, 64]),
               hm.unsqueeze(2).to_broadcast([128, 4, 64]), ALU.mult, [(ke, "qd"), "hm"], [(ke, "qh")])
            yield
            mm(P1[0:64, 0:256], o.kd, o.qh, [(ke, "kd"), (ke, "qh")], [pk[1]])
            mm(P1[:, 256:512], o.kteb, v, [(ke, "kteb"), kvk], [pk[1]])
            yield
            tt("dve", o.attm.rearrange("p (h t) -> p h t", h=4), P1[0:64, 0:256].rearrange("p (h t) -> p h t", h=4),
               T_.unsqueeze(1).to_broadcast([64, 4, 64]), ALU.mult, [pk[1], tkn], [(ke, "attm")])
            Sc = o.S[o.ns % 2]; Sk = ("S", d, o.ns % 2)
            Sn = o.S[(o.ns + 1) % 2]; Snk = ("S", d, (o.ns + 1) % 2)
            o.ns += 1
            mm(P2[0:64, 0:256], o.qd, Sc, [(ke, "qd"), Sk], [pk[2]], start=True, stop=True)
            tt("dve", o.tkv, P1[:, 256:512], bdm, ALU.mult, [pk[1], "bdm"], [(ke, "tkv")])
            stt("dve", Sn, Sc, o.gam, o.tkv, ALU.mult, ALU.add, [Sk, (ke, "gam"), (ke, "tkv")], [Snk])
            yield
            for h in range(4):
                mm(P3[0:64, 64 * h:64 * (h + 1)], o.attm[:, 64 * h:64 * (h + 1)], v[:, 64 * h:64 * (h + 1)], [(ke, "attm"), kvk], [pk[3]])
            okey = ("obuf", c)
            yield
            if first_visit:
                cp("act", obuf[:, c, :], P2[0:64, 0:256], [pk[2]], [okey])
            else:
                tt("dve", obuf[:, c, :], obuf[:, c, :], P2[0:64, 0:256], ALU.add, [okey, pk[2]], [okey])
            tt("dve", obuf[:, c, :], obuf[:, c, :], P3[0:64, 0:256], ALU.add, [okey, pk[3]], [okey])

        def epilogue(c):
            gi, c0, gn = grp_of(c)
            ci = c - c0
            okey = ("obuf", c)
            g_ = None
            ob = obuf[:, c, :]
            tt("dve", osq, ob, ob, ALU.mult, [okey], ["osq"])
            op("dve", lambda e: e.tensor_reduce(out=ss4, in_=osq.rearrange("p (h t) -> p h t", h=4), axis=AX.X, op=ALU.add),
               reads=["osq"], writes=["ss4"])
            yield
            act(ss4, ss4, AF.Sqrt, ["ss4", "eps64"], ["ss4"], bias=eps64, scale=1.0 / 64)
            recip(ss4, ss4, ["ss4"], ["ss4"])
            tt("dve", osum.rearrange("p (h t) -> p h t", h=4), ob.rearrange("p (h t) -> p h t", h=4),
               ss4.unsqueeze(2).to_broadcast([64, 4, 64]), ALU.mult, [okey, "ss4"], ["osum"])
            yield
            tt("pool", osum, osum, gngb, ALU.mult, ["osum", "gngb"], ["osum"])
            gt = gld[c % 2]; gk_ = ("gld", c % 2)
            dma("sp", gt, gtm_d[c * 64:(c + 1) * 64, 384:640], [], [gk_])
            act(sil, gt, AF.Silu, [gk_], ["sil"])
            tt("dve", osum, osum, sil, ALU.mult, ["osum", "sil"], ["osum"])
            yield
            if gi not in stgs:
                stgs[gi] = [0]
            stg = stg2[gi]; sk_ = ("gstg", gi)
            pe_ = PS[3]; pek = ("PS", 3)
            for hh in range(2):
                tr(pe_[:, 64 * hh:64 * (hh + 1)], osum[:, 128 * hh:128 * (hh + 1)], identf[0:64, 0:64], ["osum", "identf"], [pek])
            cp("act", stg[:, :, ci * 64:(ci + 1) * 64], pe_[:, 0:128].rearrange("p (h t) -> p h t", h=2), [pek], [sk_])
            stgs[gi][0] += 1
            if stgs[gi][0] == gn:
                dma("pool", mix_d[0:256, c0 * 64:(c0 + gn) * 64].rearrange("(h p) t -> p h t", p=128), stg[:, :, 0:gn * 64], [sk_], [])

        gld = [ar.f32(256, parts=64) for _ in range(2)]
        stg2 = [ar.bf16(2 * 512).rearrange("p (h t) -> p h t", h=2) for _ in range(1 + (NCH - NCC + 7) // 8)]
        fo = list(range(NCH))
        bo = list(range(NCC - 1, -1, -1)) + list(range(NCH - 1, NCC - 1, -1))
        seen = set()
        pend_epi = []

        def run_gens(gens):
            gens = list(gens)
            while gens:
                for g_ in list(gens):
                    try:
                        next(g_)
                    except StopIteration:
                        gens.remove(g_)

        for s_ in range(NCH):
            gens = []
            newdone = []
            for o, c in ((dirs[0], fo[s_]), (dirs[1], bo[s_])):
                gens.append(chunk(o, c, c not in seen))
                if c in seen:
                    newdone.append(c)
                seen.add(c)
            def epis(cs):
                for c_ in cs:
                    yield from epilogue(c_)
            if pend_epi:
                gens.append(epis(pend_epi))
            run_gens(gens)
            pend_epi = newdone
        if pend_epi:
            for c_ in pend_epi:
                for _ in epilogue(c_):
                    pass
        barrier()
    def phase_SWA(l):
        ar.reset()
        NB = TL // 128
        kT = ar.bf16(2 * T, parts=64).rearrange("p (g t) -> p g t", g=2)
        dma("sp", kT, sk_d.rearrange("(g d) t -> d g t", d=64), [], ["kT"])
        vaug = ar.bf16(NT128 * 256).rearrange("p (c g e) -> p c g e", c=NT128, g=2)
        memset("pool", vaug, 1.0, ["vaug"])
        for g_ in range(2):
            dma("sp", vaug[:, :, g_, 0:64], sv_d[:, 64 * g_:64 * (g_ + 1)].rearrange("(c p) d -> p c d", p=128), [], ["vaug"])
        esink = ar.f32(8, parts=64)
        dma("sp", esink, sink[l:l + 1, :].partition_broadcast(64), [], ["esink"])
        act(esink, esink, AF.Exp, ["esink"], ["esink"])
        mlo = ar.bf16(128); mhi = ar.bf16(128)
        tri_mask(mlo, "mlo", "ge", 128)
        tri_mask(mhi, "mhi", "le", 128)
        q8s = [ar.bf16(8 * 128, parts=64).rearrange("p (h t) -> p h t", h=8) for _ in range(2)]
        pts = [ar.bf16(512) for _ in range(4)]
        dens = [ar.f32(512, parts=64) for _ in range(2)]
        osb = [ar.bf16(512, parts=64) for _ in range(2)]
        blocks = []
        if l < L - 1:
            blocks += [(0, [(0, None), (1, None)]), (128, [(0, None), (1, None)])]
        for n in range(NB):
            kl = [(0, None), (1, None)]
            if n > 0: kl.append((2 + n - 1, "lo"))
            kl.append((2 + n, None))
            if n < NB - 1: kl.append((2 + n + 1, "hi"))
            blocks.append((CTX + 128 * n, kl))
        cnt = {"npt": 0, "nps": 0, "nd": 0}
        shwb = []
        for i_ in range(3):
            t_ = ar.f32(1152)
            dma("sp", t_, r_shw[l, i_:i_ + 1, :].partition_broadcast(128), [], [("shwb", i_)])
            shwb.append(t_)
        p0b = [[ar.f32(1152) for _ in range(4)] for _ in range(2)]
        NT0 = T // 128
        NCC0 = CTX // 128

        def p0(c):
            pm, pc, pn, sv_ = p0b[c % 2]
            kk_ = lambda nm: ("p0", c % 2, nm)
            t0 = 128 * c
            first = (c == 0 or c == NCC0)
            lastc = (c == NCC0 - 1 or c == NT0 - 1)
            dma("sp", pc, rw_d[t0:t0 + 128, :], [], [kk_("pc")])
            if first:
                memset("pool", pm, 0.0, [kk_("pm")])
                dma("sp", pm[1:128, :], rw_d[t0:t0 + 127, :], [], [kk_("pm")])
            else:
                dma("sp", pm, rw_d[t0 - 1:t0 + 127, :], [], [kk_("pm")])
            if lastc:
                memset("pool", pn, 0.0, [kk_("pn")])
                dma("sp", pn[0:127, :], rw_d[t0 + 1:t0 + 128, :], [], [kk_("pn")])
            else:
                dma("sp", pn, rw_d[t0 + 1:t0 + 129, :], [], [kk_("pn")])
            yield
            tt("pool", sv_, pm, shwb[0], ALU.mult, [kk_("pm"), ("shwb", 0)], [kk_("s")])
            tt("dve", pc, pc, shwb[1], ALU.mult, [kk_("pc"), ("shwb", 1)], [kk_("pc")])
            yield
            tt("pool", pn, pn, shwb[2], ALU.mult, [kk_("pn"), ("shwb", 2)], [kk_("pn")])
            tt("dve", sv_, sv_, pc, ALU.add, [kk_("pc"), kk_("s")], [kk_("s")])
            yield
            tt("pool", sv_, sv_, pn, ALU.add, [kk_("pn"), kk_("s")], [kk_("s")])
            dma("pool", rws_d[t0:t0 + 128, :], sv_, [kk_("s")], [])
            yield

        p0_next = [0]

        def p0_some(n):
            for _ in range(n):
                if p0_next[0] < NT0:
                    c_ = p0_next[0]; p0_next[0] += 1
                    yield from p0(c_)

        def block(tq0, kl, bi):
            q8 = q8s[bi % 2]; qk = ("q8", bi % 2)
            dma("sp", q8, sq_d[:, tq0:tq0 + 128].rearrange("(h d) t -> d h t", d=64), [], [qk])
            for g in range(2):
                po = PS[4 + 2 * (bi % 2) + g]; pok = ("PS", 4 + 2 * (bi % 2) + g)
                for ik, (c, msk) in enumerate(kl):
                    nps = cnt["nps"]; cnt["nps"] += 1; ps = PS[nps % 4]; psk = ("PS", nps % 4)
                    mm(ps, kT[:, g, c * 128:(c + 1) * 128], q8[:, 4 * g:4 * g + 4, :], ["kT", qk], [psk])
                    npt = cnt["npt"]; cnt["npt"] += 1; pt = pts[npt % 4]; ptk = ("pt", npt % 4)
                    act(pt, ps, AF.Exp, [psk], [ptk])
                    if msk is not None:
                        mt_ = mlo if msk == "lo" else mhi
                        tt("pool" if ik % 2 else "dve", pt.rearrange("p (h t) -> p h t", h=4), pt.rearrange("p (h t) -> p h t", h=4),
                           mt_.unsqueeze(1).to_broadcast([128, 4, 128]), ALU.mult, [ptk, "m" + msk], [ptk])
                    mm(po, vaug[:, c, g, :], pt, ["vaug", ptk], [pok], start=(ik == 0), stop=(ik == len(kl) - 1))
                    yield
                nd = cnt["nd"]; cnt["nd"] += 1
                den = dens[nd % 2]; dk = ("den", nd % 2)
                ob = osb[nd % 2]; obk = ("osb", nd % 2)
                yield
                cp("act", den, po[64:128, :], [pok], [dk])
                tt("dve", den.rearrange("p (h t) -> p h t", h=4), den.rearrange("p (h t) -> p h t", h=4),
                   esink[:, 4 * g:4 * g + 4].unsqueeze(2).to_broadcast([64, 4, 128]), ALU.add, [dk, "esink"], [dk])
                recip(den, den, [dk], [dk])
                tt("dve", ob, po[0:64, :], den, ALU.mult, [pok, dk], [obk])
                dma("pool", mix_d[256 + 256 * g:256 + 256 * (g + 1), tq0:tq0 + 128].rearrange("(h d) t -> d h t", d=64),
                    ob.rearrange("p (h t) -> p h t", h=4), [obk], [])
        def run_gens(gens):
            gens = list(gens)
            while gens:
                for g_ in list(gens):
                    try:
                        next(g_)
                    except StopIteration:
                        gens.remove(g_)

        for b0 in range(0, len(blocks), 2):
            run_gens([block(blocks[b0 + i][0], blocks[b0 + i][1], b0 + i) for i in range(2) if b0 + i < len(blocks)] + [p0_some(2)])
        run_gens([p0_some(NT0)])
        barrier()
    RC = 128
    rwA_d = dscr("rwA", [T // 64, 2, 64, 1024])
    rwE_d = dscr("rwE", [T // 64, 64, 512])
    CDEC = 0.6065306597126334

    def phase_RWKV(l):
        ar.reset()
        NCH = T // RC
        NCC = CTX // RC
        NCH64 = T // 64
        NCC64 = CTX // 64
        def h4(x):
            return x.rearrange("p (h d) -> p h d", h=4)
        mk = {}
        for kind in ("lt", "le", "gt", "ge"):
            t_ = ar.f32(128)
            tri_mask(t_, "r_" + kind, kind, 128)
            mk[kind] = t_
        BD = ar.f32(128)
        memset("pool", BD, 0.0, ["BD"])
        memset("pool", BD[0:64, 0:64], 1.0, ["BD"])
        memset("pool", BD[64:128, 64:128], 1.0, ["BD"])
        for kind in ("lt", "le", "gt", "ge"):
            tt("dve", mk[kind], mk[kind], BD, ALU.mult, ["r_" + kind, "BD"], ["r_" + kind])
        sel = ar.f32(2)
        memset("pool", sel, 0.0, ["sel"])
        memset("pool", sel[0:64, 0:1], 1.0, ["sel"])
        memset("pool", sel[64:128, 1:2], 1.0, ["sel"])
        BS = {0: "lt", 1: "gt"}; BI = {0: "le", 1: "ge"}; BST = {0: "gt", 1: "lt"}; REM = {0: "gt", 1: "lt"}
        MA = []; MB = []; MC = []
        for d in range(2):
            ma = ar.f32(256); mb = ar.f32(256); mc = ar.f32(128)
            ts("dve", ma[:, 0:128], mk[BS[d]], -1.0, None, ALU.mult, None, ["r_" + BS[d]], [("MA", d)])
            cp("dve", ma[:, 128:256], mk[BI[d]], ["r_" + BI[d]], [("MA", d)])
            cp("dve", mb[:, 0:128], mk[BS[d]], ["r_" + BS[d]], [("MB", d)])
            cp("dve", mb[:, 128:256], mk[BI[d]], ["r_" + BI[d]], [("MB", d)])
            ts("dve", mc, mk[BST[d]], -1.0, None, ALU.mult, None, ["r_" + BST[d]], [("MC", d)])
            MA.append(ma); MB.append(mb); MC.append(mc)
        id64 = identf[0:64, 0:64]
        def bc(src_ap, n, key):
            t_ = ar.f32(n)
            dma("sp", t_, src_ap.partition_broadcast(128), [], [key])
            return t_
        kkb = bc(r_kk[l:l + 1, :], 256, "kkb"); kab = bc(r_ka[l:l + 1, :], 256, "kab"); rkb = bc(r_rk[l:l + 1, :], 256, "rkb")
        lngb = bc(r_lng[l:l + 1, :], 256, "lngb"); lnbb = bc(r_lnb[l:l + 1, :], 256, "lnbb")
        w0b = ar.f32(512); a0b = ar.f32(512)
        for d in range(2):
            dma("sp", w0b[:, 256 * d:256 * (d + 1)], r_w0[l, d:d + 1, :].partition_broadcast(128), [], ["w0b"])
            dma("sp", a0b[:, 256 * d:256 * (d + 1)], r_a0[l, d:d + 1, :].partition_broadcast(128), [], ["a0b"])
        W2blk = ar.f32(512); A2blk = ar.f32(512); g2t = ar.f32(256)
        memset("pool", W2blk, 0.0, ["W2blk"]); memset("pool", A2blk, 0.0, ["A2blk"])
        for d in range(2):
            dma("sp", W2blk[64 * d:64 * (d + 1), 256 * d:256 * (d + 1)], r_w2[l, d], [], ["W2blk"])
            dma("sp", A2blk[64 * d:64 * (d + 1), 256 * d:256 * (d + 1)], r_a2[l, d], [], ["A2blk"])
        dma("sp", g2t, r_g2[l], [], ["g2t"])
        e12 = ar.f32(1); egn = ar.f32(1)
        memset("pool", e12, 1e-12, ["e12"]); memset("pool", egn, 64e-5, ["egn"])

        mark_slots = ar.off

        def mkslot(si):
            o = K()
            o.si = si
            o.s = ar.f32(1152)
            o.kk = ar.f32(256); o.t256 = ar.f32(256); o.s4 = ar.f32(4)
            o.E = ar.f32(512)
            o.sx = ar.f32(384)
            o.TT = ar.f32(384)
            o.sig = ar.f32(512); o.aa = ar.f32(512)
            o.kd = ar.f32(256); o.b = ar.f32(256)
            o.ei = ar.bf16(256); o.en = ar.bf16(256); o.ex = ar.bf16(256); o.er = ar.bf16(256)
            o.gam = ar.f32(8, parts=64)
            o.Rp = ar.bf16(256); o.KKp = ar.bf16(256); o.Bm = ar.bf16(256); o.KDm = ar.bf16(256)
            o.Bend = ar.bf16(256); o.KDend = ar.bf16(256); o.vb = ar.bf16(256)
            o.Bs = [ar.bf16(256), ar.bf16(256)]; o.KDs = [ar.bf16(256), ar.bf16(256)]
            o.KR = ar.bf16(1024, parts=64); o.BK = ar.bf16(1024, parts=64)
            o.SA = ar.bf16(1024); o.SB = ar.bf16(1024); o.Y0 = ar.bf16(512)
            o.Pc = [ar.bf16(512), ar.bf16(512)]
            o.YY = [ar.bf16(1024), ar.bf16(1024)]
            o.RHSK = ar.bf16(512); o.KU = ar.bf16(512); o.Dg = ar.f32(512, parts=64)
            o.OUTa = ar.f32(2 * 768, parts=64); o.OUTb = ar.f32(256)
            return o

        def K_(o, nm):
            return ("rs", o.si, nm)

        gnp = [0]

        def nb(o):
            i = gnp[0] % 8; gnp[0] += 1
            return PS[i], ("PS", i)

        def p1(c, o):
            t0 = RC * c
            first = (c == 0 or c == NCC)
            lastc = (c == NCC - 1 or c == NCH - 1)
            k_ = lambda nm: (("rsh", nm) if nm in ("pm", "pc", "pn") else K_(o, nm))
            dma("sp", o.s, rws_d[t0:t0 + RC, :], [], [k_("s")])
            yield
            S = o.s; sk_ = k_("s")
            r = S[:, 0:256]; k = S[:, 256:512]; v = S[:, 512:768]
            cp("act", o.vb, v, [sk_], [k_("vb")])
            vb = o.vb; vbk = k_("vb")
            tt("pool", o.kk, k, kkb, ALU.mult, [sk_, "kkb"], [k_("kk")])
            tt("dve", o.t256, o.kk, o.kk, ALU.mult, [k_("kk")], [k_("t256")])
            op("dve", lambda e: e.tensor_reduce(out=o.s4, in_=h4(o.t256), axis=AX.X, op=ALU.add), reads=[k_("t256")], writes=[k_("s4")])
            act(o.s4, o.s4, AF.Sqrt, [k_("s4"), "e12"], [k_("s4")], bias=e12)
            recip(o.s4, o.s4, [k_("s4")], [k_("s4")])
            tt("dve", h4(o.kk), h4(o.kk), o.s4.unsqueeze(2).to_broadcast([128, 4, 64]), ALU.mult, [k_("kk"), k_("s4")], [k_("kk")])
            yield
            tt("pool", o.t256, r, k, ALU.mult, [sk_, k_("t256")], [k_("t256")])
            tt("pool", o.t256, o.t256, rkb, ALU.mult, [k_("t256"), "rkb"], [k_("t256")])
            op("dve", lambda e: e.tensor_reduce(out=o.s4, in_=h4(o.t256), axis=AX.X, op=ALU.add), reads=[k_("t256")], writes=[k_("s4")])
            tt("dve", h4(o.E[:, 0:256]), h4(v), o.s4.unsqueeze(2).to_broadcast([128, 4, 64]), ALU.mult, [sk_, k_("s4")], [k_("E")])
            act(o.sx[:, 0:128], S[:, 1024:1152], AF.Sigmoid, [sk_], [k_("sx")])
            act(o.sx[:, 128:256], S[:, 768:896], AF.Tanh, [sk_], [k_("sx")])
            cp("pool", o.sx[:, 256:384], S[:, 896:1024], [sk_], [k_("sx")])
            pb, pbk = nb(o)
            for i in range(3):
                tr(pb[:, 128 * i:128 * (i + 1)], o.sx[:, 128 * i:128 * (i + 1)], identf, [k_("sx"), "identf"], [pbk])
            cp("act", o.TT, pb[:, 0:384], [pbk], [k_("TT")])
            yield
            pg, pgk = nb(o)
            mm(pg[:, 0:256], o.TT[:, 0:128], g2t, [k_("TT"), "g2t"], [pgk])
            cp("act", o.E[:, 256:512], pg[:, 0:256], [pgk], [k_("E")])
            dma("pool", rwE_d[2 * c:2 * c + 2].rearrange("s p n -> (s p) n"), o.E, [k_("E")], [])
            pz, pzk = nb(o)
            mm(pz, o.TT[:, 128:256], W2blk, [k_("TT"), "W2blk"], [pzk])
            tt("dve", o.sig, pz, w0b, ALU.add, [pzk, "w0b"], [k_("sig")])
            act(o.sig, o.sig, AF.Sigmoid, [k_("sig")], [k_("sig")])
            pa, pak = nb(o)
            mm(pa, o.TT[:, 256:384], A2blk, [k_("TT"), "A2blk"], [pak])
            tt("dve", o.aa, pa, a0b, ALU.add, [pak, "a0b"], [k_("aa")])
            act(o.aa, o.aa, AF.Sigmoid, [k_("aa")], [k_("aa")])
            yield
            for d in range(2):
                sg = o.sig[:, 256 * d:256 * (d + 1)]; a = o.aa[:, 256 * d:256 * (d + 1)]
                stt("dve", o.kd, a, -1.0, kab, ALU.add, ALU.mult, [k_("aa"), "kab"], [k_("kd")])
                stt("dve", o.kd, o.kd, 1.0, k, ALU.add, ALU.mult, [k_("kd"), sk_], [k_("kd")])
                tt("pool", o.b, a, o.kk, ALU.mult, [k_("aa"), k_("kk")], [k_("b")])
                pi, pik = nb(o)
                mm(pi[:, 0:256], mk[BI[d]], sg, ["r_" + BI[d], k_("sig")], [pik])
                mm(pi[:, 256:512], mk[BS[d]], sg, ["r_" + BS[d], k_("sig")], [pik])
                pr, prk = nb(o)
                mm(pr[:, 0:256], mk[REM[d]], sg, ["r_" + REM[d], k_("sig")], [prk])
                for h in range(4):
                    mm(pr[0:64, 256 + 2 * h:258 + 2 * h], sg[:, 64 * h:64 * (h + 1)], sel, [k_("sig"), "sel"], [prk])
                act(o.ei, pi[:, 0:256], AF.Exp, [pik], [k_("ei")], scale=-CDEC)
                act(o.en, pi[:, 0:256], AF.Exp, [pik], [k_("en")], scale=CDEC)
                act(o.ex, pi[:, 256:512], AF.Exp, [pik], [k_("ex")], scale=-CDEC)
                act(o.er, pr[:, 0:256], AF.Exp, [prk], [k_("er")], scale=-CDEC)
                act(o.gam, pr[0:64, 256:264], AF.Exp, [prk], [k_("gam")], scale=-CDEC)
                yield
                tt("dve", o.Rp, r, o.ei, ALU.mult, [sk_, k_("ei")], [k_("Rp")])
                tt("pool", o.KKp, o.kk, o.ex, ALU.mult, [k_("kk"), k_("ex")], [k_("KKp")])
                tt("dve", o.Bm, o.b, o.en, ALU.mult, [k_("b"), k_("en")], [k_("Bm")])
                tt("pool", o.KDm, o.kd, o.en, ALU.mult, [k_("kd"), k_("en")], [k_("KDm")])
                tt("dve", o.Bend, o.b, o.er, ALU.mult, [k_("b"), k_("er")], [k_("Bend")])
                tt("pool", o.KDend, o.kd, o.er, ALU.mult, [k_("kd"), k_("er")], [k_("KDend")])
                for s_ in range(2):
                    ts("dve", o.Bs[s_], o.Bend, sel[:, s_:s_ + 1], None, ALU.mult, None, [k_("Bend"), "sel"], [k_("Bs")])
                    ts("dve", o.KDs[s_], o.KDend, sel[:, s_:s_ + 1], None, ALU.mult, None, [k_("KDend"), "sel"], [k_("KDs")])
                px_, pxk = nb(o)
                py_, pyk = nb(o)
                px = px_.bitcast(BF16); py = py_.bitcast(BF16)
                for h in range(4):
                    hs = slice(64 * h, 64 * (h + 1))
                    tr(px[0:64, 256 * h:256 * h + 128], o.KKp[:, hs], identb, [k_("KKp"), "identb"], [pxk])
                    tr(px[0:64, 256 * h + 128:256 * h + 256], o.Rp[:, hs], identb, [k_("Rp"), "identb"], [pxk])
                    tr(py[0:64, 128 * h:128 * h + 128], o.Bm[:, hs], identb, [k_("Bm"), "identb"], [pyk])
                    tr(py[0:64, 512 + 128 * h:512 + 128 * h + 128], o.KDm[:, hs], identb, [k_("KDm"), "identb"], [pyk])
                cp("act", o.KR, px[0:64, :], [pxk], [k_("KR")])
                cp("dve", o.BK, py[0:64, :], [pyk], [k_("BK")])
                yield
                KR3 = o.KR.rearrange("p (h n) -> p h n", h=4)
                pA = []; pB = []
                for i in range(2):
                    pA.append(nb(o)); pB.append(nb(o))
                pC, pCk = nb(o)
                for h in range(4):
                    BmT = o.BK[:, 128 * h:128 * (h + 1)]; KDmT = o.BK[:, 512 + 128 * h:512 + 128 * (h + 1)]
                    mm(pA[h // 2][0][:, 256 * (h % 2):256 * (h % 2 + 1)], BmT, KR3[:, h, :], [k_("BK"), k_("KR")], [pA[h // 2][1]])
                    mm(pB[h // 2][0][:, 256 * (h % 2):256 * (h % 2 + 1)], KDmT, KR3[:, h, :], [k_("BK"), k_("KR")], [pB[h // 2][1]])
                    mm(pC[:, 128 * h:128 * (h + 1)], KR3[:, h, 0:128], BmT, [k_("BK"), k_("KR")], [pCk])
                SA3 = o.SA.rearrange("p (h n) -> p h n", h=4); SB3 = o.SB.rearrange("p (h n) -> p h n", h=4)
                Y03 = o.Y0.rearrange("p (h n) -> p h n", h=4)
                for i in range(2):
                    tt("dve", SA3[:, 2 * i:2 * i + 2, :], pA[i][0].rearrange("p (h n) -> p h n", h=2), MA[d].unsqueeze(1).to_broadcast([128, 2, 256]),
                       ALU.mult, [pA[i][1], ("MA", d)], [k_("SA")])
                    tt("dve", SB3[:, 2 * i:2 * i + 2, :], pB[i][0].rearrange("p (h n) -> p h n", h=2), MB[d].unsqueeze(1).to_broadcast([128, 2, 256]),
                       ALU.mult, [pB[i][1], ("MB", d)], [k_("SB")])
                tt("dve", Y03, pC.rearrange("p (h n) -> p h n", h=4), MC[d].unsqueeze(1).to_broadcast([128, 4, 128]), ALU.mult, [pCk, ("MC", d)], [k_("Y0")])
                yield
                Pc = o.Pc[0]; Pk = k_("P0")
                P3 = lambda P_: P_.rearrange("p (h n) -> p h n", h=4)
                tt("pool", P3(Pc), SA3[:, :, 0:128], identf.unsqueeze(1).to_broadcast([128, 4, 128]), ALU.add, [k_("SA"), "identf"], [Pk])
                Yt = lambda h: SA3[:, h, 0:128]
                Yn = lambda h: Y03[:, h, :]
                ytk = k_("SA"); ynk = k_("Y0")
                for i in range(1, 6):
                    pq = [nb(o), nb(o)]
                    for h in range(4):
                        pq_, pqk_ = pq[h // 2]
                        if i < 5:
                            mm(pq_[:, 256 * (h % 2):256 * (h % 2) + 128], Yn(h), Yt(h), [ytk, ynk], [pqk_])
                        mm(pq_[:, 256 * (h % 2) + 128:256 * (h % 2) + 256], Yt(h), Yn(h), [ytk, ynk], [pqk_])
                    YY = o.YY[i % 2]; yyk = k_("YY%d" % (i % 2))
                    YY3 = YY.rearrange("p (h n) -> p h n", h=4)
                    for j in range(2):
                        cp("act", YY3[:, 2 * j:2 * j + 2, :], pq[j][0].rearrange("p (h n) -> p h n", h=2), [pq[j][1]], [yyk])
                    Yt = (lambda YY3: (lambda h: YY3[:, h, 0:128]))(YY3)
                    Yn = (lambda YY3: (lambda h: YY3[:, h, 128:256]))(YY3)
                    ytk = yyk; ynk = yyk
                    pp, ppk = nb(o)
                    for h in range(4):
                        mm(pp[:, 128 * h:128 * (h + 1)], Yn(h), Pc[:, 128 * h:128 * (h + 1)], [yyk, Pk], [ppk])
                    Pn = o.Pc[i % 2]; Pnk = k_("P%d" % (i % 2))
                    tt("dve", Pn, Pc, pp, ALU.add, [Pk, ppk], [Pnk])
                    Pc = Pn; Pk = Pnk
                    yield
                RH3 = o.RHSK.rearrange("p (h n) -> p h n", h=4)
                pw, pwk = nb(o)
                for h in range(4):
                    mm(pw[:, 64 * h:64 * (h + 1)], SB3[:, h, 0:128], vb[:, 64 * h:64 * (h + 1)], [k_("SB"), vbk], [pwk])
                ts("dve", RH3[:, :, 64:128], h4(pw[:, 0:256]), -1.0, None, ALU.mult, None, [pwk], [k_("RHSK")])
                cp("pool", RH3[:, :, 0:64], h4(o.KKp), [k_("KKp")], [k_("RHSK")])
                pu, puk = nb(o)
                for h in range(4):
                    mm(pu[:, 128 * h:128 * (h + 1)], Pc[:, 128 * h:128 * (h + 1)], RH3[:, h, :], [Pk, k_("RHSK")], [puk])
                cp("act", o.KU, pu, [puk], [k_("KU")])
                KU3 = o.KU.rearrange("p (h n) -> p h n", h=4)
                yield
                OA = o.OUTa.rearrange("p (s n) -> p s n", s=2)
                gam3 = o.gam.rearrange("p (h s) -> p h s", s=2)
                pg2, pg2k = nb(o)
                ph, phk = nb(o)
                for s_ in range(2):
                    for h in range(4):
                        hs = slice(64 * h, 64 * (h + 1))
                        cs = slice(256 * s_ + 64 * h, 256 * s_ + 64 * (h + 1))
                        mm(pg2[0:64, cs], KU3[:, h, 0:64], o.Bs[s_][:, hs], [k_("KU"), k_("Bs")], [pg2k])
                        mm(ph[0:64, cs], o.Bs[s_][:, hs], KU3[:, h, 64:128], [k_("KU"), k_("Bs")], [phk], start=True, stop=False)
                        mm(ph[0:64, cs], o.KDs[s_][:, hs], vb[:, hs], [k_("KDs"), vbk], [phk], start=False, stop=True)
                Dg3 = o.Dg.rearrange("p (s n) -> p s n", s=2)
                for s_ in range(2):
                    tt("pool", h4(Dg3[:, s_, :]), id64.unsqueeze(1).to_broadcast([64, 4, 64]), gam3[:, :, s_].unsqueeze(2).to_broadcast([64, 4, 64]),
                       ALU.mult, ["identf", k_("gam")], [k_("Dg")])
                    tt("dve", OA[:, s_, 0:256], Dg3[:, s_, :], pg2[0:64, 256 * s_:256 * (s_ + 1)], ALU.subtract, [k_("Dg"), pg2k], [k_("OUTa")])
                    cp("act", OA[:, s_, 256:512], ph[0:64, 256 * s_:256 * (s_ + 1)], [phk], [k_("OUTa")])
                prt, prtk = nb(o)
                for h in range(4):
                    mm(prt[0:64, 128 * h:128 * (h + 1)], KU3[:, h, 0:64], SA3[:, h, 128:256], [k_("KU"), k_("SA")], [prtk])
                prt3 = prt[0:64, :].rearrange("p (h n) -> p h n", h=4)
                for s_ in range(2):
                    tt("dve", OA[:, s_, 512:768].rearrange("p (h n) -> p h n", h=4), KR3[:, :, 128 + 64 * s_:128 + 64 * (s_ + 1)],
                       prt3[:, :, 64 * s_:64 * (s_ + 1)], ALU.subtract, [k_("KR"), prtk], [k_("OUTa")])
                po, pok = nb(o)
                for h in range(4):
                    hs = slice(64 * h, 64 * (h + 1))
                    mm(po[:, hs], SA3[:, h, 128:256], KU3[:, h, 64:128], [k_("KU"), k_("SA")], [pok], start=True, stop=False)
                    mm(po[:, hs], SB3[:, h, 128:256], vb[:, hs], [k_("SB"), vbk], [pok], start=False, stop=True)
                cp("act", o.OUTb, po[:, 0:256], [pok], [k_("OUTb")])
                for s_ in range(2):
                    dma("pool", rwA_d[2 * c + s_, d, :, 0:768], OA[:, s_, :], [k_("OUTa")], [])
                    dma("pool", rwA_d[2 * c + s_, d, :, 768:1024], o.OUTb[64 * s_:64 * (s_ + 1), :], [k_("OUTb")], [])
                yield

        def run_gens(gens):
            gens = list(gens)
            while gens:
                for g_ in list(gens):
                    try:
                        next(g_)
                    except StopIteration:
                        gens.remove(g_)

        NSL = 3
        slots = [mkslot(i) for i in range(NSL)]
        for c in range(0, NCH, NSL):
            run_gens([p1(c + i, slots[i]) for i in range(NSL) if c + i < NCH])
        barrier()
        import os
        if os.environ.get("RSTOP") == "1":
            return
        ar.off = mark_slots
        NCH = NCH64; NCC = NCC64
        lngb64 = lngb[0:64, :]; lnbb64 = lnbb[0:64, :]
        obuf = ar.f32(NCH * 256, parts=64).rearrange("p (c n) -> p c n", c=NCH)
        ds = []
        for d in range(2):
            o = K(); o.d = d
            o.A = [ar.f32(1024, parts=64) for _ in range(2)]
            o.Z = [ar.f32(256, parts=64) for _ in range(2)]
            o.n = 0
            memset("pool", o.Z[0], 0.0, [("Z", d, 0)])
            ds.append(o)
        Eb = [ar.f32(512, parts=64) for _ in range(4)]
        NP3 = 4
        p3b = [(ar.f32(256, parts=64), ar.f32(256, parts=64), ar.f32(4, parts=64), ar.f32(4, parts=64)) for _ in range(NP3)]
        NG = 1 + (NCH - NCC + 7) // 8
        stgs = [ar.bf16(2 * 512).rearrange("p (h t) -> p h t", h=2) for _ in range(NG)]
        scnt = [0] * NG

        def grp_of(c):
            return (0, 0, NCC) if c < NCC else (1 + (c - NCC) // 8, NCC + ((c - NCC) // 8) * 8, 8)

        def p2(o, c, first_visit):
            d = o.d
            A = o.A[o.n % 2]; Ak = ("A", d, o.n % 2)
            Zc = o.Z[o.n % 2]; Zk = ("Z", d, o.n % 2)
            Zn = o.Z[(o.n + 1) % 2]; Znk = ("Z", d, (o.n + 1) % 2)
            o.n += 1
            dma("sp", A, rwA_d[c, d], [], [Ak])
            A4 = A.rearrange("p (m n) -> p m n", m=4)
            pO = PS[4 * d]; pOk = ("PS", 4 * d); pZ = PS[4 * d + 1 + (o.n % 2)]; pZk = ("PS", 4 * d + 1 + (o.n % 2))
            for h in range(4):
                hs = slice(64 * h, 64 * (h + 1))
                mm(pZ[0:64, hs], A4[:, 0, hs], Zc[:, hs], [Ak, Zk], [pZk])
            tt("dve", Zn, pZ[0:64, 0:256], A4[:, 1, :], ALU.add, [pZk, Ak], [Znk])
            for h in range(4):
                hs = slice(64 * h, 64 * (h + 1))
                mm(pO[0:64, hs], A4[:, 2, hs], Zc[:, hs], [Ak, Zk], [pOk])
            okey = ("obuf", c)
            if first_visit:
                tt("pool" if False else "dve", obuf[:, c, :], pO[0:64, 0:256], A4[:, 3, :], ALU.add, [pOk, Ak], [okey])
            else:
                tt("dve", A4[:, 3, :], pO[0:64, 0:256], A4[:, 3, :], ALU.add, [pOk, Ak], [Ak])
                tt("pool", obuf[:, c, :], obuf[:, c, :], A4[:, 3, :], ALU.add, [okey, Ak], [okey])

        def p3(c):
            gi, c0, gn = grp_of(c)
            ci = c - c0
            okey = ("obuf", c)
            ob = obuf[:, c, :]
            cen, sq, m4, v4 = p3b[c % NP3]
            kc_ = lambda nm: (nm, c % NP3)
            E = Eb[c % NP3]; Ek = ("Eb", c % NP3)
            dma("sp", E, rwE_d[c], [], [Ek])
            op("dve", lambda e: e.tensor_reduce(out=m4, in_=h4(ob), axis=AX.X, op=ALU.add), reads=[okey], writes=[kc_("m4")])
            yield
            ts("dve", m4, m4, 1.0 / 64, None, ALU.mult, None, [kc_("m4")], [kc_("m4")])
            yield
            tt("dve", h4(cen), h4(ob), m4.unsqueeze(2).to_broadcast([64, 4, 64]), ALU.subtract, [okey, kc_("m4")], [kc_("cen")])
            yield
            tt("pool", sq, cen, cen, ALU.mult, [kc_("cen")], [kc_("sq")])
            yield
            op("dve", lambda e: e.tensor_reduce(out=v4, in_=h4(sq), axis=AX.X, op=ALU.add), reads=[kc_("sq")], writes=[kc_("v4")])
            yield
            act(v4, v4, AF.Sqrt, [kc_("v4"), "egn"], [kc_("v4")], bias=egn[0:64, :], scale=1.0 / 64)
            yield
            recip(v4, v4, [kc_("v4")], [kc_("v4")])
            yield
            tt("dve", h4(cen), h4(cen), v4.unsqueeze(2).to_broadcast([64, 4, 64]), ALU.mult, [kc_("cen"), kc_("v4")], [kc_("cen")])
            yield
            tt("pool", cen, cen, lngb64, ALU.mult, [kc_("cen"), "lngb"], [kc_("cen")])
            yield
            tt("pool", cen, cen, lnbb64, ALU.add, [kc_("cen"), "lnbb"], [kc_("cen")])
            yield
            tt("dve", cen, cen, E[:, 0:256], ALU.add, [kc_("cen"), Ek], [kc_("cen")])
            yield
            tt("dve", cen, cen, E[:, 256:512], ALU.mult, [kc_("cen"), Ek], [kc_("cen")])
            yield
            stg = stgs[gi]; sk_ = ("rstg", gi)
            pe_ = PS[3 + 4 * (c % 2)]; pek = ("PS", 3 + 4 * (c % 2))
            for hh in range(2):
                tr(pe_[:, 64 * hh:64 * (hh + 1)], cen[:, 128 * hh:128 * (hh + 1)], id64, [kc_("cen"), "identf"], [pek])
            cp("act", stg[:, :, ci * 64:(ci + 1) * 64], pe_[:, 0:128].rearrange("p (h t) -> p h t", h=2), [pek], [sk_])
            yield
            scnt[gi] += 1
            if scnt[gi] == gn:
                dma("pool", mix_d[768:1024, c0 * 64:(c0 + gn) * 64].rearrange("(h p) t -> p h t", p=128), stg[:, :, 0:gn * 64], [sk_], [])

        fo = list(range(NCH))
        bo = list(range(NCC - 1, -1, -1)) + list(range(NCH - 1, NCC - 1, -1))
        seen = set()
        for s_ in range(NCH):
            for o, c in ((ds[0], fo[s_]), (ds[1], bo[s_])):
                p2(o, c, c not in seen)
                seen.add(c)
        for c0_ in range(0, NCH, NP3):
            run_gens([p3(c0_ + i) for i in range(NP3) if c0_ + i < NCH])
        barrier()
    def phase_FFN(l):
        ar.reset()
        last = (l == L - 1)
        src = xin if l == 0 else xs
        wdb = ar.bf16(NFC * D).rearrange("p (f c) -> p f c", f=NFC)
        wob = ar.bf16(8 * D).rearrange("p (k c) -> p k c", k=8)
        g1b = [ar.f32(D) for _ in range(2)]
        g2b = [ar.f32(D) for _ in range(2)]
        for ti in range(2):
            dma("sp", g1b[ti], gates[l, ti:ti + 1, 0:1024].partition_broadcast(128), [], [("g1b", ti)])
            dma("sp", g2b[ti], gates[l, ti:ti + 1, 1024:2048].partition_broadcast(128), [], [("g2b", ti)])
        if last:
            fngb = ar.f32(D)
            dma("sp", fngb, fng.rearrange("(o d) -> o d", o=1).partition_broadcast(128), [], ["fngb"])
        mark = ar.off
        stg = [ar.f32(2048) for _ in range(2)]
        cstg = [ar.bf16(2048) for _ in range(2)]
        ns = 0
        engs3 = ("dve", "act", "pool")
        for kc in range(8):
            s_ = stg[ns % 2]; sk_ = ("stg", ns % 2)
            dma(("sp", "pool")[ns % 2], s_[:, 0:D], w_out[l, kc * 128:(kc + 1) * 128, :], [], [sk_])
            cp(engs3[ns % 3], wob[:, kc, :], s_[:, 0:D], [sk_], [("wob", kc)])
            ns += 1
        for f0 in range(0, NFC, 2):
            s_ = stg[ns % 2]; sk_ = ("stg", ns % 2)
            dma(("sp", "pool")[ns % 2], s_.rearrange("p (f c) -> p f c", f=2), wd[l, f0 * 128:(f0 + 2) * 128, :].rearrange("(f p) c -> p f c", p=128), [], [sk_])
            cp(engs3[ns % 3], wdb[:, f0:f0 + 2, :], s_.rearrange("p (f c) -> p f c", f=2), [sk_], [("wdb", f0), ("wdb", f0 + 1)])
            ns += 1
        for wsrc, wdst in ((wg, wgb_d), (wu, wub_d)):
            for f0 in range(0, NFC, 2):
                s_ = stg[ns % 2]; sk_ = ("stg", ns % 2)
                c_ = cstg[ns % 2]; ck_ = ("cstg", ns % 2)
                dma("sp", s_.rearrange("p (k c) -> p k c", k=8), wsrc[l, :, f0 * 128:(f0 + 2) * 128].rearrange("(k p) c -> p k c", p=128), [], [sk_])
                cp(engs3[ns % 3], c_.rearrange("p (f k c) -> p k f c", f=2, k=8), s_.rearrange("p (k f c) -> p k f c", k=8, f=2), [sk_], [ck_])
                dma("pool", wdst[f0:f0 + 2].rearrange("f p c -> p f c"), c_.rearrange("p (f c) -> p f c", f=2), [ck_], ["wgu_d"])
                ns += 1
        barrier()
        ar.off = mark
        wobk = [("wob", kc) for kc in range(8)]
        h2T = ar.bf16(8 * 512).rearrange("p (k c) -> p k c", k=8)
        actT = ar.bf16(NFC * 512).rearrange("p (f c) -> p f c", f=NFC)
        x1 = [ar.f32(D) for _ in range(4)]
        xts = [ar.f32(D) for _ in range(2)]
        mts = [ar.bf16(8 * 128).rearrange("p (k c) -> p k c", k=8) for _ in range(2)]
        junk = ar.bf16(D); ss = ar.f32(1); rstd = ar.f32(1); xn = ar.bf16(D)
        PT = PS[7].bitcast(BF16)
        scr = (junk, ss, rstd, xn, PT)
        tmpf = [ar.f32(512) for _ in range(2)]
        sgs = [ar.f32(512) for _ in range(2)]
        NR = 3
        wgt = [ar.bf16(1024).rearrange("p (k c) -> p k c", k=8) for _ in range(NR)]
        wut = [ar.bf16(1024).rearrange("p (k c) -> p k c", k=8) for _ in range(NR)]
        nx = 0; ntmp = 0; nring = 0; nsg = 0; nyo = 0
        for (t0, NT, ti) in tiles():
            if last and ti == 1:
                continue
            nsub = NT // 128
            for j in range(nsub):
                tok = slice(t0 + 128 * j, t0 + 128 * (j + 1))
                xt = xts[nx % 2]; xk = ("xt", nx % 2)
                mt = mts[nx % 2]; mk = ("mt", nx % 2); nx += 1
                dma("sp", xt, src[tok, :], [], [xk])
                dma("sp", mt, mix_d[:, tok].rearrange("(k p) t -> p k t", p=128), ["mix_d"], [mk])
                for h in range(2):
                    ps = PS[h]; pk = ("PS", h)
                    for kc in range(8):
                        mm(ps, mt[:, kc, :], wob[:, kc, h * 512:(h + 1) * 512], [mk, wobk[kc]], [pk], start=(kc == 0), stop=(kc == 7))
                    tf = tmpf[ntmp % 2]; tk = ("tmpf", ntmp % 2); ntmp += 1
                    tt("dve", tf, ps, g1b[ti][:, h * 512:(h + 1) * 512], ALU.mult, [pk, ("g1b", ti)], [tk])
                    tt("pool", x1[j][:, h * 512:(h + 1) * 512], xt[:, h * 512:(h + 1) * 512], tf, ALU.add, [xk, tk], [("x1", j)])
                norm_mod_T(x1[j], ("x1", j), A2, B2, l, ti, h2T, ("h2T", j), 128 * j, scr)
            hk = [("h2T", j) for j in range(nsub)]
            for fc in range(NFC):
                r_ = nring % NR; nring += 1
                gk_ = ("wgt", r_); uk_ = ("wut", r_)
                dma("sp", wgt[r_], wgb_d[fc].rearrange("p (k c) -> p k c", k=8), ["wgu_d"], [gk_])
                dma("sp", wut[r_], wub_d[fc].rearrange("p (k c) -> p k c", k=8), ["wgu_d"], [uk_])
                pg = PS[2 + fc % 2]; pgk = ("PS", 2 + fc % 2)
                pu = PS[4 + fc % 2]; puk = ("PS", 4 + fc % 2)
                for kc in range(8):
                    mm(pg[:, :NT], wgt[r_][:, kc, :], h2T[:, kc, :NT], [gk_] + hk, [pgk], start=(kc == 0), stop=(kc == 7))
                for kc in range(8):
                    mm(pu[:, :NT], wut[r_][:, kc, :], h2T[:, kc, :NT], [uk_] + hk, [puk], start=(kc == 0), stop=(kc == 7))
                sg = sgs[nsg % 2]; sgk = ("sg", nsg % 2); nsg += 1
                act(sg[:, :NT], pg[:, :NT], AF.Silu, [pgk], [sgk])
                tt("dve", actT[:, fc, :NT], sg[:, :NT], pu[:, :NT], ALU.mult, [sgk, puk], [("actT", fc)])
            ak = [("actT", fc) for fc in range(NFC)]
            for j in range(nsub):
                tok = slice(t0 + 128 * j, t0 + 128 * (j + 1))
                y = x1[j]; yk = ("x1", j)
                for h in range(2):
                    ps = PS[h]; pk = ("PS", h)
                    for fc in range(NFC):
                        mm(ps, actT[:, fc, 128 * j:128 * (j + 1)], wdb[:, fc, h * 512:(h + 1) * 512], [ak[fc], ("wdb", fc)], [pk],
                           start=(fc == 0), stop=(fc == NFC - 1))
                    tf = tmpf[ntmp % 2]; tk = ("tmpf", ntmp % 2); ntmp += 1
                    tt("dve", tf, ps, g2b[ti][:, h * 512:(h + 1) * 512], ALU.mult, [pk, ("g2b", ti)], [tk])
                    tt("pool", y[:, h * 512:(h + 1) * 512], x1[j][:, h * 512:(h + 1) * 512], tf, ALU.add, [("x1", j), tk], [yk])
                if not last:
                    dma("pool", xs[tok, :], y, [yk], [])
                else:
                    act(junk, y, AF.Square, [yk], ["junk", "ss"], accum=ss)
                    act(rstd, ss, AF.Sqrt, ["ss", "epsc"], ["rstd"], bias=epsc, scale=1.0 / D)
                    recip(rstd, rstd, ["rstd"], ["rstd"])
                    stt("dve", y, y, rstd, fngb, ALU.mult, ALU.mult, [yk, "rstd", "fngb"], [yk])
                    dma("pool", out[t0 - CTX + 128 * j:t0 - CTX + 128 * (j + 1), :], y, [yk], [])
        barrier()
    phase_ada()
    for l in range(L):
        if "P" in phases:
            phase_P(l)
        if "GLA" in phases:
            phase_GLA(l)
        if "SWA" in phases:
            phase_SWA(l)
        if "RWKV" in phases:
            phase_RWKV(l)
        if "FFN" in phases:
            phase_FFN(l)
    fw.emit()
    es.close()
    return nc


def host_inputs(inp, b, TL, L):
    f = lambda a: np.ascontiguousarray(np.asarray(a, dtype=np.float32))
    m = {}
    m["xin"] = f(np.concatenate([inp["ctx"][b], inp["x"][b][:TL]], axis=0))
    cvec = np.stack([np.asarray(inp["c"][b]), np.asarray(inp["c_ctx"])], axis=-1)
    m["cc"] = f(cvec.reshape(8, 128, 2).transpose(1, 0, 2))
    return m


def shared_inputs(inp, TL, L):
    f = lambda a: np.ascontiguousarray(np.asarray(a, dtype=np.float32))
    m = {}
    for nm in ("ada_w", "ada_b", "w_out", "gla_decay_w", "gla_decay_b", "gla_norm_g", "swa_sink", "rwkv_shift_w",
               "rwkv_w0", "rwkv_w2", "rwkv_a0", "rwkv_a2", "rwkv_g2", "rwkv_k_k", "rwkv_k_a", "rwkv_ln_g", "rwkv_ln_b",
               "ffn_w_gate", "ffn_w_up", "ffn_w_down"):
        m[nm] = f(np.asarray(inp[nm])[:L])
    m["rwkv_r_k"] = f(np.asarray(inp["rwkv_r_k"])[:L].reshape(L, 256))
    m["final_norm_g"] = f(inp["final_norm_g"])
    m["nmg"] = f(np.asarray(inp["norm_mix_g"])[:L].reshape(L, 8, 128).transpose(0, 2, 1))
    m["nfg"] = f(np.asarray(inp["norm_ffn_g"])[:L].reshape(L, 8, 128).transpose(0, 2, 1))
    m["wr"] = relayout_w_in(f(np.asarray(inp["w_in"])[:L]))
    c_, s_ = rope_tables(TL)
    m["cosT"] = c_; m["sinT"] = s_
    return m


_CACHE = {}


def kernel(**inputs):
    TL, L, NCORE = 4096, 4, 8
    if "nc" not in _CACHE:
        _CACHE["nc"] = build(TL, L)
    nc = _CACHE["nc"]
    sh = shared_inputs(inputs, TL, L)
    in_maps = []
    for b in range(NCORE):
        m = dict(sh)
        m.update(host_inputs(inputs, b, TL, L))
        in_maps.append(m)
    res = run_bass_kernel_spmd(nc, in_maps, core_ids=list(range(NCORE)))
    return np.stack([np.asarray(r["out"], dtype=np.float32) for r in res.results], axis=0)
```

```python
import concourse.bass as bass
import concourse.mybir as mybir


class _Ins:
    __slots__ = ("eng", "fn", "dma", "deps", "signal", "sig", "sem", "val", "tag")

    def __init__(self, eng, fn, dma):
        self.eng = eng
        self.fn = fn
        self.dma = dma
        self.deps = ()
        self.signal = False
        self.sig = 0
        self.sem = None
        self.val = 0
        self.tag = None


class FW:
    NDMA = 8

    def __init__(self, nc):
        self.nc = nc
        self.engs = {"pe": nc.tensor, "dve": nc.vector, "act": nc.scalar,
                     "pool": nc.gpsimd, "sp": nc.sync}
        self.order = []
        self.state = {}
        self.last = {}
        self.pend = []

    def op(self, eng, fn, reads=(), writes=(), dma=False, tag=None):
        ins = _Ins(eng, fn, dma)
        ins.tag = tag
        if tag == "f32":
            ins.signal = True
        deps = {}
        st = self.state
        for k in reads:
            s = st.get(k)
            if s is not None and s[0] is not None:
                w = s[0]
                if w.dma or w.eng != eng or eng != "pe":
                    deps[id(w)] = w
        for k in writes:
            s = st.get(k)
            if s is not None:
                w = s[0]
                if w is not None and (w.dma or w.eng != eng or eng != "pe"):
                    deps[id(w)] = w
                for r in s[1].values():
                    if r.dma or r.eng != eng:
                        deps[id(r)] = r
                for r in s[2]:
                    deps[id(r)] = r
        ins.deps = tuple(deps.values())
        for d in ins.deps:
            d.signal = True
        for k in reads:
            s = st.get(k)
            if s is None:
                s = st[k] = [None, {}, []]
            if dma:
                s[2].append(ins)
            else:
                s[1][eng] = ins
        for k in writes:
            st[k] = [ins, {}, []]
        self.order.append(ins)
        if dma:
            self.pend.append(ins)
        else:
            self.last[eng] = ins
        return ins

    def barrier(self):
        deps = list(self.last.values()) + self.pend
        for d in deps:
            d.signal = True
        for e in self.engs:
            b = _Ins(e, None, False)
            b.deps = tuple(deps)
            self.order.append(b)
        self.pend = []
        self.last = {}
        self.state = {}

    def emit(self, final_waits=()):
        nc = self.nc
        import contextlib
        with contextlib.ExitStack() as es:
            esem = {e: es.enter_context(nc.semaphore("s_" + e)) for e in self.engs}
            dsem = {e: [es.enter_context(nc.semaphore("d_%s%d" % (e, i))) for i in range(self.NDMA)]
                    for e in self.engs}
            cnt = {e: 0 for e in self.engs}
            dcnt = {e: 0 for e in self.engs}
            dslot = {e: [None] * self.NDMA for e in self.engs}
            waited = {e: {} for e in self.engs}
            last_f32 = [None]

            def wait(eng, sem, val):
                w = waited[eng]
                k = id(sem)
                if w.get(k, 0) >= val:
                    return
                w[k] = val
                self.engs[eng].wait_ge(sem, val)

            for ins in self.order:
                e = ins.eng
                if ins.dma:
                    i = dcnt[e]
                    dcnt[e] += 1
                    slot = i % self.NDMA
                    prev = dslot[e][slot]
                    if prev is not None:
                        wait(e, prev.sem, prev.val)
                    ins.sem = dsem[e][slot]
                    ins.val = 16 * (i // self.NDMA + 1)
                    dslot[e][slot] = ins
                elif ins.signal:
                    cnt[e] += 1
                    ins.sem = esem[e]
                    ins.val = cnt[e]
                for d in ins.deps:
                    wait(e, d.sem, d.val)
                if ins.fn is None:
                    continue
                if ins.tag == "fwl" and last_f32[0] is not None:
                    wait(e, last_f32[0].sem, last_f32[0].val)
                    last_f32[0] = None
                elif ins.tag == "f32":
                    last_f32[0] = ins
                bi = ins.fn(self.engs[e])
                if ins.dma:
                    bi.then_inc(ins.sem, 16)
                elif ins.signal:
                    bi.then_inc(ins.sem, 1)
                ins.fn = None
            for e in self.engs:
                for prev in dslot[e]:
                    if prev is not None:
                        wait(e, prev.sem, prev.val)
            self.counts = (cnt, dcnt)
import numpy as np
import contextlib
import concourse.bass as bass
import concourse.mybir as mybir
from concourse.bass_utils import run_bass_kernel_spmd

F32 = mybir.dt.float32
BF16 = mybir.dt.bfloat16
ALU = mybir.AluOpType
AF = mybir.ActivationFunctionType
AX = mybir.AxisListType

D = 1024
CTX = 256
NFM = 12
LFB0 = NFM * 128
TM0 = LFB0 + 32
NTM = 1920
WR = TM0 + NTM
DFF = 2816
NFC = 22
GLA_SCALE = 32 ** -0.5


def relayout_w_in(w_in):
    L = w_in.shape[0]
    g0, s0, r0 = 0, 800, 1568
    gq = w_in[:, :, 0:128]; gk = w_in[:, :, 128:256]; gv = w_in[:, :, 256:512]; gg = w_in[:, :, 512:768]
    lfb = w_in[:, :, 768:800]
    sq = w_in[:, :, s0:s0 + 512]; sk = w_in[:, :, s0 + 512:s0 + 640]; sv = w_in[:, :, s0 + 640:s0 + 768]
    rw = w_in[:, :, r0:r0 + 1152]

    def swap(a):
        sh = a.shape
        a = a.reshape(sh[0], sh[1], sh[2] // 64, 2, 32)
        return a[:, :, :, ::-1, :].reshape(sh)

    return np.ascontiguousarray(np.concatenate(
        [gq, gk, sq, swap(sq), sk, swap(sk), lfb, gk, gv, gg, sv, rw], axis=2))


def rope_tables(TL):
    rows = TL // 64
    row = np.repeat(np.arange(rows, dtype=np.float32), 64)
    col = np.tile(np.arange(64, dtype=np.float32), rows)
    inv = (10000.0 ** (-np.arange(16, dtype=np.float32) / 16)).astype(np.float32)
    ang = np.concatenate([row[:, None] * inv, col[:, None] * inv], axis=-1)
    cos = np.cos(ang).astype(np.float32).T
    sin = np.sin(ang).astype(np.float32).T
    cosT = np.concatenate([np.ones((32, CTX), np.float32), cos], axis=1)
    sinT = np.concatenate([np.zeros((32, CTX), np.float32), sin], axis=1)
    cos128 = np.concatenate([cosT, cosT, cosT, cosT], axis=0)
    sin128 = np.concatenate([-sinT, sinT, -sinT, sinT], axis=0)
    return np.ascontiguousarray(cos128), np.ascontiguousarray(sin128)


class K:
    pass


def build(TL, DEPTH, dbg=(), phases=('P', 'GLA', 'SWA', 'RWKV', 'FFN')):
    T = CTX + TL
    NT128 = T // 128
    nc = bass.Bass("TRN2", target_bir_lowering=False)
    fw = FW(nc)
    es = contextlib.ExitStack()

    def din(name, shape, dt=F32):
        return nc.dram_tensor(name, list(shape), dt, kind="ExternalInput").ap()

    def dscr(name, shape, dt=F32):
        kind = "ExternalOutput" if name in dbg else "Internal"
        return nc.dram_tensor(name, list(shape), dt, kind=kind).ap()

    L = DEPTH
    xin = din("xin", [T, D])
    cc = din("cc", [128, 8, 2])
    ada_w = din("ada_w", [L, D, 6 * D])
    ada_b = din("ada_b", [L, 6 * D])
    nmg = din("nmg", [L, 128, 8])
    nfg = din("nfg", [L, 128, 8])
    wr = din("wr", [L, D, WR])
    w_out = din("w_out", [L, D, D])
    gdw = din("gla_decay_w", [L, 2, 16, 128])
    gdb = din("gla_decay_b", [L, 2, 128])
    gng = din("gla_norm_g", [L, 64])
    sink = din("swa_sink", [L, 8])
    r_shw = din("rwkv_shift_w", [L, 3, 1152])
    r_w0 = din("rwkv_w0", [L, 2, 256])
    r_w2 = din("rwkv_w2", [L, 2, 64, 256])
    r_a0 = din("rwkv_a0", [L, 2, 256])
    r_a2 = din("rwkv_a2", [L, 2, 64, 256])
    r_g2 = din("rwkv_g2", [L, 128, 256])
    r_kk = din("rwkv_k_k", [L, 256])
    r_ka = din("rwkv_k_a", [L, 256])
    r_rk = din("rwkv_r_k", [L, 256])
    r_lng = din("rwkv_ln_g", [L, 256])
    r_lnb = din("rwkv_ln_b", [L, 256])
    wg = din("ffn_w_gate", [L, D, DFF])
    wu = din("ffn_w_up", [L, D, DFF])
    wd = din("ffn_w_down", [L, DFF, D])
    fng = din("final_norm_g", [D])
    cosd = din("cosT", [128, T])
    sind = din("sinT", [128, T])
    out = nc.dram_tensor("out", [TL, D], F32, kind="ExternalOutput").ap()

    xs = dscr("xs", [T, D])
    gates = dscr("gates", [L, 2, 2048])
    gq_d = dscr("gq", [128, T]); gk_d = dscr("gk", [128, T])
    gtm_d = dscr("gtm", [T, 640])
    glog_d = dscr("glog", [T, 256])
    sq_d = dscr("sq", [512, T], BF16); sk_d = dscr("sk", [128, T], BF16)
    sv_d = dscr("sv", [T, 128], BF16)
    rw_d = dscr("rw", [T, 1152])
    rws_d = dscr("rws", [T, 1152])
    mix_d = din("mixT", [D, T], BF16) if "mixT_in" in dbg else dscr("mixT", [D, T], BF16)
    wgb_d = dscr("wgb", [NFC, 128, 8 * 128], BF16)
    wub_d = dscr("wub", [NFC, 128, 8 * 128], BF16)
    wdb_d = dscr("wdb", [NFC, 128, D], BF16)

    ARENA_F32 = 52 * 1024 - 512
    arena_t = es.enter_context(nc.sbuf_tensor("arena", [128, ARENA_F32], F32))

    class Arena:
        def __init__(s, lo, hi):
            s.lo = lo; s.hi = hi; s.off = lo
        def reset(s):
            s.off = s.lo
        def f32(s, n, parts=128):
            v = arena_t[0:parts, s.off:s.off + n]
            s.off += n
            assert s.off <= s.hi, ("arena overflow", s.off, s.hi)
            return v
        def bf16(s, n, parts=128):
            m = (n + 1) // 2
            v = arena_t[0:parts, s.off:s.off + m].bitcast(BF16)
            s.off += m
            assert s.off <= s.hi, ("arena overflow", s.off, s.hi)
            return v

    PERS = 3072
    pers = Arena(0, PERS)
    ar = Arena(PERS, ARENA_F32)
    PS = [es.enter_context(nc.psum_tensor("ps%d" % i, [128, 512], F32))[:, :] for i in range(8)]

    op = fw.op

    def pe_tag(wap):
        if wap.dtype == F32:
            return "f32"
        return "fwl" if wap.shape[-1] >= 128 else None

    def mm(out, lhsT, rhs, r, w, start=True, stop=True, tp=None):
        kw = {"tile_position": tp} if tp is not None else {}
        op("pe", lambda e: e.matmul(out, lhsT=lhsT, rhs=rhs, start=start, stop=stop, **kw), reads=r, writes=w, tag=pe_tag(lhsT))

    def tr(out, in_, ident, r, w):
        op("pe", lambda e: e.transpose(out, in_, ident), reads=r, writes=w, tag=pe_tag(in_))

    def dma(eng, out, in_, r, w, **kw):
        op(eng, lambda e: e.dma_start(out=out, in_=in_, **kw), reads=r, writes=w, dma=True)

    def act(out, in_, func, r, w, bias=None, scale=None, accum=None):
        kw = {}
        if bias is not None: kw["bias"] = bias
        if scale is not None: kw["scale"] = scale
        if accum is not None: kw["accum_out"] = accum
        op("act", lambda e: e.activation(out=out, in_=in_, func=func, **kw), reads=r, writes=w)

    def tt(eng, out, in0, in1, aop, r, w):
        op(eng, lambda e: e.tensor_tensor(out=out, in0=in0, in1=in1, op=aop), reads=r, writes=w)

    def ts(eng, out, in0, s1, s2, op0, op1, r, w, accum=None):
        kw = {}
        if op1 is not None: kw["op1"] = op1
        if accum is not None: kw["accum_out"] = accum
        op(eng, lambda e: e.tensor_scalar(out=out, in0=in0, scalar1=s1, scalar2=s2, op0=op0, **kw), reads=r, writes=w)

    def stt(eng, out, in0, scalar, in1, op0, op1, r, w):
        op(eng, lambda e: e.scalar_tensor_tensor(out=out, in0=in0, scalar=scalar, in1=in1, op0=op0, op1=op1), reads=r, writes=w)

    def cp(eng, out, in_, r, w):
        if eng == "act":
            op("act", lambda e: e.activation(out=out, in_=in_, func=AF.Copy), reads=r, writes=w)
        else:
            op(eng, lambda e: e.tensor_copy(out=out, in_=in_), reads=r, writes=w)

    def memset(eng, ap, val, w):
        op(eng, lambda e: e.memset(ap, val), writes=w)

    def recip(out, in_, r, w):
        op("dve", lambda e: e.reciprocal(out=out, in_=in_), reads=r, writes=w)

    def barrier():
        fw.barrier()

    def tri_mask(t_, key, kind, n):
        cm, st, base = {"ge": (1, -1, 0), "le": (-1, 1, 0), "gt": (1, -1, -1), "lt": (-1, 1, -1)}[kind]
        memset("pool", t_, 1.0, [key])
        op("pool", lambda e: e.affine_select(out=t_, in_=t_, pattern=[[st, n]], compare_op=ALU.is_ge,
                                             fill=0.0, base=base, channel_multiplier=cm), reads=[key], writes=[key])

    identb = pers.bf16(128)
    identf = pers.f32(128)
    ones1 = pers.f32(128, parts=1)
    epsc = pers.f32(1)
    for nm, t_ in (("identb", identb), ("identf", identf)):
        memset("pool", t_, 1.0, [nm])
        op("pool", (lambda t_: (lambda e: e.affine_select(out=t_, in_=t_, pattern=[[-1, 128]], compare_op=ALU.is_equal,
                                                         fill=0.0, base=0, channel_multiplier=1)))(t_), reads=[nm], writes=[nm])
    memset("pool", ones1, 1.0, ["ones1"])
    memset("pool", epsc, 1e-6, ["epsc"])
    onec = pers.f32(1)
    memset("pool", onec, 1.0, ["onec"])
    A1 = pers.f32(L * 16).rearrange("p (l k t) -> p l k t", l=L, k=8)
    B1 = pers.f32(L * 16).rearrange("p (l k t) -> p l k t", l=L, k=8)
    A2 = pers.f32(L * 16).rearrange("p (l k t) -> p l k t", l=L, k=8)
    B2 = pers.f32(L * 16).rearrange("p (l k t) -> p l k t", l=L, k=8)

    def phase_ada():
        ar.reset()
        cs_raw = ar.f32(16).rearrange("p (k t) -> p k t", k=8)
        cs = ar.f32(16).rearrange("p (k t) -> p k t", k=8)
        dma("sp", cs_raw, cc, [], ["cs_raw"])
        act(cs, cs_raw, AF.Silu, ["cs_raw"], ["cs"])
        nmg_t = ar.f32(L * 8).rearrange("p (l k) -> p l k", l=L)
        nfg_t = ar.f32(L * 8).rearrange("p (l k) -> p l k", l=L)
        dma("sp", nmg_t, nmg.rearrange("l p k -> p l k"), [], ["nmg_t"])
        dma("sp", nfg_t, nfg.rearrange("l p k -> p l k"), [], ["nfg_t"])
        modfm = ar.f32(96).rearrange("p (j t) -> p j t", t=2)
        tmp = ar.f32(16).rearrange("p (k t) -> p k t", k=8)
        abrow = ar.f32(6 * D, parts=1)
        awt = [ar.f32(8 * 1024).rearrange("p (k c) -> p k c", k=8) for _ in range(2)]
        grow = ar.f32(2048, parts=2)
        PSa = PS[0][:, 0:96].rearrange("p (j t) -> p j t", t=2)
        for l in range(L):
            dma("sp", abrow, ada_b[l:l + 1, :], [], ["abrow"])
            for s in range(6):
                a = awt[s % 2]
                ak = ("awt", s % 2)
                for kh in range(2):
                    dma(("sp", "pool")[kh], a[:, 4 * kh:4 * kh + 4, :],
                        ada_w[l, 512 * kh:512 * (kh + 1), s * 1024:(s + 1) * 1024].rearrange("(k p) c -> p k c", p=128), [], [ak])
                for jj in range(8):
                    j = s * 8 + jj
                    for kc in range(8):
                        mm(PSa[:, j, :], a[:, kc, jj * 128:(jj + 1) * 128], cs[:, kc, :], [ak, "cs"], ["PSa"], start=(kc == 0), stop=False)
                    mm(PSa[:, j, :], abrow[0:1, j * 128:(j + 1) * 128], ones1[0:1, 0:2], ["abrow", "ones1"], ["PSa"], start=False, stop=True)
                if s in (2, 5):
                    for h in range(2):
                        pg = PS[1 + h][0:2, :]
                        pk = ("PSgt", h)
                        for kc in range(8):
                            mm(pg, cs[:, kc, :], a[:, kc, h * 512:(h + 1) * 512], [ak, "cs"], [pk], start=(kc == 0), stop=False)
                        mm(pg, ones1[0:1, 0:2], abrow[0:1, s * 1024 + h * 512: s * 1024 + (h + 1) * 512], ["abrow", "ones1"], [pk], start=False, stop=True)
                        o0 = (0 if s == 2 else 1024) + h * 512
                        cp("dve", grow[:, o0:o0 + 512], pg, [pk], ["grow"])
            dma("pool", gates[l], grow, ["grow"], [("gates", l)])
            cp("dve", modfm, PSa, ["PSa"], ["modfm"])
            for (Ax, Bx, gt, gk_, s_sc, s_sh) in ((A1, B1, nmg_t, "nmg_t", 1, 0), (A2, B2, nfg_t, "nfg_t", 4, 3)):
                ts("dve", tmp, modfm[:, s_sc * 8:(s_sc + 1) * 8, :], 1.0, None, ALU.add, None, ["modfm"], ["adatmp"])
                tt("dve", Ax[:, l], tmp, gt[:, l, :].unsqueeze(2).to_broadcast([128, 8, 2]), ALU.mult, ["adatmp", gk_], ["mod"])
                cp("dve", Bx[:, l], modfm[:, s_sh * 8:(s_sh + 1) * 8, :], ["modfm"], ["mod"])
        barrier()

    def norm_mod_T(xt, xk, A, B, l, ti, hT, hk, col0, scr):
        junk, ss, rstd, xn, PT = scr
        act(junk, xt, AF.Square, [xk], ["junk", "ss"], accum=ss)
        act(rstd, ss, AF.Sqrt, ["ss", "epsc"], ["rstd"], bias=epsc, scale=1.0 / D)
        recip(rstd, rstd, ["rstd"], ["rstd"])
        ts("dve", xn, xt, rstd, None, ALU.mult, None, [xk, "rstd"], ["xn"])
        for kc in range(8):
            tr(PT[:, kc * 128:(kc + 1) * 128], xn[:, kc * 128:(kc + 1) * 128], identb, ["xn", "identb"], ["PT"])
        for kc in range(8):
            act(hT[:, kc, col0:col0 + 128], PT[:, kc * 128:(kc + 1) * 128], AF.Identity, ["PT", "mod"], [hk],
                bias=B[:, l, kc, ti:ti + 1], scale=A[:, l, kc, ti:ti + 1])

    def tiles():
        res = [(0, CTX, 1)]
        t0 = CTX
        while t0 < T:
            res.append((t0, 512, 0))
            t0 += 512
        return res

    def phase_P(l):
        ar.reset()
        src = xin if l == 0 else xs
        wbf = ar.bf16(8 * WR).rearrange("p (k c) -> p k c", k=8)
        cosTs = [ar.f32(512) for _ in range(2)]; sinTs = [ar.f32(512) for _ in range(2)]
        ncs = 0
        wdaug = ar.f32(256, parts=64)
        memset("pool", wdaug, 0.0, ["wdaug"])
        for d_ in range(2):
            dma("sp", wdaug[32 * d_:32 * d_ + 16, 128 * d_:128 * (d_ + 1)], gdw[l, d_], [], ["wdaug"])
            dma("sp", wdaug[32 * d_ + 16:32 * d_ + 17, 128 * d_:128 * (d_ + 1)], gdb[l, d_:d_ + 1, :], [], ["wdaug"])
        lfbT = ar.f32(512, parts=64)
        memset("pool", lfbT, 1.0, ["lfbT"])
        stg = [ar.f32(WR) for _ in range(2)]
        for kc in range(8):
            sk_ = ("stg", kc % 2)
            dma(("sp", "pool")[kc % 2], stg[kc % 2], wr[l, kc * 128:(kc + 1) * 128, :], [], [sk_])
            cp(("dve", "act", "pool")[kc % 3], wbf[:, kc, :], stg[kc % 2], [sk_], [("wbf", kc)])
        wk = [("wbf", kc) for kc in range(8)]
        hT = ar.bf16(8 * 512).rearrange("p (k c) -> p k c", k=8)
        xts = [ar.f32(D) for _ in range(3)]
        junk = ar.bf16(D); ss = ar.f32(1); rstd = ar.f32(1); xn = ar.bf16(D)
        PT = PS[7].bitcast(BF16)
        scr = (junk, ss, rstd, xn, PT)
        fmo = [ar.f32(512) for _ in range(2)]
        t1s = [ar.f32(512) for _ in range(2)]; t2s = [ar.f32(512) for _ in range(2)]
        rob = [ar.bf16(512) for _ in range(2)]
        tmrow = [ar.f32(NTM) for _ in range(2)]
        svb = [ar.bf16(128) for _ in range(2)]
        ge = [ar.f32(256) for _ in range(2)]
        nx = 0; nfm = 0; nro = 0; ntm = 0
        for (t0, NT, ti) in tiles():
            nsub = NT // 128
            cosT = cosTs[ncs % 2]; sinT = sinTs[ncs % 2]; csk = ("cosT", ncs % 2); snk = ("sinT", ncs % 2); ncs += 1
            dma("sp", cosT[:, :NT], cosd[:, t0:t0 + NT], [], [csk]); dma("sp", sinT[:, :NT], sind[:, t0:t0 + NT], [], [snk])
            for j in range(nsub):
                xt = xts[nx % 3]; xk = ("xt", nx % 3); nx += 1
                dma("sp", xt, src[t0 + 128 * j:t0 + 128 * (j + 1), :], [], [xk])
                norm_mod_T(xt, xk, A1, B1, l, ti, hT, ("hT", j), 128 * j, scr)
            hk = [("hT", j) for j in range(nsub)]
            import os
            PSTOP = int(os.environ.get("PSTOP", "9"))
            if PSTOP <= 1:
                continue
            def fm_mm(c, ps, pk):
                for kc in range(8):
                    mm(ps[:, :NT], wbf[:, kc, c * 128:(c + 1) * 128], hT[:, kc, :NT], [wk[kc]] + hk, [pk], start=(kc == 0), stop=(kc == 7))
            for c, dst in ((0, gq_d), (1, gk_d)):
                ps = PS[c]; pk = ("PS", c)
                fm_mm(c, ps, pk)
                o = fmo[nfm % 2]; ok_ = ("fmo", nfm % 2); nfm += 1
                cp("act", o[:, :NT], ps[:, :NT], [pk], [ok_])
                dma("pool", dst[:, t0:t0 + NT], o[:, :NT], [ok_], [])
            for idx, (c, c2, dst, r0, scl) in enumerate([(2, 6, sq_d, 0, 0.125), (3, 7, sq_d, 128, 0.125), (4, 8, sq_d, 256, 0.125),
                                                         (5, 9, sq_d, 384, 0.125), (10, 11, sk_d, 0, 1.0)]):
                pa = PS[(2 * idx) % 4]; pka = ("PS", (2 * idx) % 4)
                pb = PS[(2 * idx + 1) % 4]; pkb = ("PS", (2 * idx + 1) % 4)
                fm_mm(c, pa, pka); fm_mm(c2, pb, pkb)
                t1 = t1s[nro % 2]; t2 = t2s[nro % 2]; ob = rob[nro % 2]
                k1 = ("t1", nro % 2); k2 = ("t2", nro % 2); kob = ("rob", nro % 2); nro += 1
                stt("dve", t1[:, :NT], pa[:, :NT], scl, cosT[:, :NT], ALU.mult, ALU.mult, [pka, csk], [k1])
                stt("dve", t2[:, :NT], pb[:, :NT], scl, sinT[:, :NT], ALU.mult, ALU.mult, [pkb, snk], [k2])
                tt("pool", ob[:, :NT], t1[:, :NT], t2[:, :NT], ALU.add, [k1, k2], [kob])
                dma("pool", dst[r0:r0 + 128, t0:t0 + NT], ob[:, :NT], [kob], [])
            if PSTOP <= 2:
                continue
            pl = PS[2]; pkl = ("PS", 2)
            for kc in range(8):
                mm(pl[0:16, :NT], wbf[:, kc, LFB0:LFB0 + 16], hT[:, kc, :NT], [wk[kc]] + hk, [pkl], start=(kc == 0), stop=(kc == 7))
            for kc in range(8):
                mm(pl[32:48, :NT], wbf[:, kc, LFB0 + 16:LFB0 + 32], hT[:, kc, :NT], [wk[kc]] + hk, [pkl], start=(kc == 0), stop=(kc == 7), tp=(0, 32))
            cp("act", lfbT[0:16, :NT], pl[0:16, :NT], [pkl], ["lfbT"])
            cp("act", lfbT[32:48, :NT], pl[32:48, :NT], [pkl], ["lfbT"])
            if PSTOP <= 3:
                continue
            for j in range(nsub):
                tm = tmrow[ntm % 2]; tk = ("tmrow", ntm % 2)
                sv_ = svb[ntm % 2]; svk = ("svb", ntm % 2)
                g_ = ge[ntm % 2]; gk_ = ("ge", ntm % 2); ntm += 1
                tok = slice(t0 + 128 * j, t0 + 128 * (j + 1))
                for gi, (o, n) in enumerate(((0, 512), (512, 512), (1024, 512), (1536, 384))):
                    ps = PS[4 + gi % 3]; pk = ("PS", 4 + gi % 3)
                    for kc in range(8):
                        mm(ps[:, :n], hT[:, kc, 128 * j:128 * (j + 1)], wbf[:, kc, TM0 + o:TM0 + o + n], [wk[kc], hk[j]], [pk],
                           start=(kc == 0), stop=(kc == 7))
                    cp(("act", "dve")[gi % 2], tm[:, o:o + n], ps[:, :n], [pk], [tk])
                cp("pool", sv_, tm[:, 640:768], [tk], [svk])
                dma("pool", gtm_d[tok, :], tm[:, 0:640], [tk], [])
                dma("pool", sv_d[tok, :], sv_, [svk], [])
                dma("pool", rw_d[tok, :], tm[:, 768:1920], [tk], [])
                if PSTOP <= 4:
                    continue
                pg = PS[4]; pk = ("PS", 4)
                mm(pg[:, 0:256], lfbT[0:64, 128 * j:128 * (j + 1)], wdaug[0:64, :], ["lfbT", "wdaug"], [pk])
                if PSTOP <= 5:
                    cp("act", g_, pg[:, 0:256], [pk], [gk_])
                else:
                    act(g_, pg[:, 0:256], AF.Exp, [pk], [gk_], scale=-1.0)
                    act(g_, g_, AF.Ln, [gk_, "onec"], [gk_], bias=onec)
                ts("dve", g_, g_, -1.0 / 16.0, None, ALU.mult, None, [gk_], [gk_])
                dma("pool", glog_d[tok, :], g_, [gk_], [])
        barrier()
    def phase_GLA(l):
        ar.reset()
        NCH = T // 64
        NCC = CTX // 64
        tri = {}
        for nm, kind in (("tf", "le"), ("tb", "ge"), ("rf", "gt"), ("rb", "lt")):
            t_ = ar.f32(64, parts=64)
            tri_mask(t_, "g_" + nm, kind, 64)
            tri[nm] = t_
        Eh = ar.f32(128, parts=4); Ev = ar.f32(256, parts=4)
        for t_, nm, w_, n_ in ((Eh, "Eh", 32, 128), (Ev, "Ev", 64, 256)):
            memset("pool", t_, 1.0, [nm])
            op("pool", (lambda t_, w_, n_: (lambda e: e.affine_select(out=t_, in_=t_, pattern=[[1, n_]], compare_op=ALU.is_ge,
                                                                      fill=0.0, base=0, channel_multiplier=-w_)))(t_, w_, n_), reads=[nm], writes=[nm])
            op("pool", (lambda t_, w_, n_: (lambda e: e.affine_select(out=t_, in_=t_, pattern=[[-1, n_]], compare_op=ALU.is_ge,
                                                                      fill=0.0, base=w_ - 1, channel_multiplier=w_)))(t_, w_, n_), reads=[nm], writes=[nm])
        bdm = ar.f32(256); hm = ar.f32(4)
        mm(PS[0][:, 0:256], Eh, Ev, ["Eh", "Ev"], [("PS", 0)])
        mm(PS[1][:, 0:4], Eh, identf[0:4, 0:4], ["Eh", "identf"], [("PS", 1)])
        cp("dve", bdm, PS[0][:, 0:256], [("PS", 0)], ["bdm"])
        cp("dve", hm, PS[1][:, 0:4], [("PS", 1)], ["hm"])
        gngb = ar.f32(256, parts=64)
        for h in range(4):
            dma("sp", gngb[:, 64 * h:64 * (h + 1)], gng[l:l + 1, :].partition_broadcast(64), [], ["gngb"])
        eps64 = ar.f32(1, parts=64)
        memset("pool", eps64, 1e-6, ["eps64"])
        obuf = ar.f32(NCH * 256, parts=64).rearrange("p (c n) -> p c n", c=NCH)
        dirs = []
        for d in range(2):
            o = K()
            o.d = d
            o.qTg = ar.f32(512); o.kTg = ar.f32(512)
            o.gtm = ar.f32(8 * 640, parts=64).rearrange("p (c n) -> p c n", c=8)
            o.glg = ar.f32(8 * 128, parts=64).rearrange("p (c n) -> p c n", c=8)
            o.eb = ar.f32(64); o.enb = ar.f32(64); o.qd = ar.f32(64); o.kd = ar.bf16(64); o.gam = ar.f32(1)
            o.kte = ar.f32(128, parts=64); o.kteb = ar.bf16(128, parts=64)
            o.vbg = ar.bf16(8 * 256, parts=64).rearrange("p (c n) -> p c n", c=8)
            o.qh = ar.bf16(256); o.attm = ar.bf16(256, parts=64); o.tkv = ar.f32(256)
            o.S = [ar.f32(256), ar.f32(256)]
            o.ns = 0
            o.grp = None
            memset("pool", o.S[0], 0.0, [("S", d, 0)])
            dirs.append(o)
        osum = ar.f32(256, parts=64); osq = ar.f32(256, parts=64); ss4 = ar.f32(4, parts=64); sil = ar.f32(256, parts=64)
        stgs = {}
        PB = {0: (PS[0], PS[1], PS[2], PS[3]), 1: (PS[4], PS[5], PS[6], PS[7])}

        def grp_of(c):
            return (0, 0, NCC) if c < NCC else (1 + (c - NCC) // 8, NCC + ((c - NCC) // 8) * 8, 8)

        def chunk(o, c, first_visit):
            d = o.d
            gi, c0, gn = grp_of(c)
            kq = ("qTg", d); kg = ("gtm", d); kl = ("glg", d)
            if o.grp != gi:
                o.grp = gi
                t0g = c0 * 64
                dma("sp", o.qTg[:, 0:gn * 64], gq_d[:, t0g:t0g + gn * 64], [], [kq])
                dma("sp", o.kTg[:, 0:gn * 64], gk_d[:, t0g:t0g + gn * 64], [], [kq])
                dma("sp", o.gtm[:, 0:gn, :], gtm_d[t0g:t0g + gn * 64, :].rearrange("(c p) n -> p c n", p=64), [], [kg])
                dma("sp", o.glg[:, 0:gn, :], glog_d[t0g:t0g + gn * 64, 128 * d:128 * (d + 1)].rearrange("(c p) n -> p c n", p=64), [], [kl])
                cp("pool", o.vbg[:, 0:gn, :], o.gtm[:, 0:gn, 128:384], [kg], [("vbg", d)])
            ci = c - c0
            qT = o.qTg[:, ci * 64:(ci + 1) * 64]; kT = o.kTg[:, ci * 64:(ci + 1) * 64]
            ktok = o.gtm[:, ci, 0:128]; v = o.vbg[:, ci, :]; kvk = ("vbg", d)
            glog = o.glg[:, ci, :]
            P0, P1, P2, P3 = PB[d]
            pk = [("PS", 4 * d + i) for i in range(4)]
            T_ = tri["tf" if d == 0 else "tb"]; R_ = tri["rf" if d == 0 else "rb"]
            tkn = "g_tf" if d == 0 else "g_tb"; rkn = "g_rf" if d == 0 else "g_rb"
            mm(P0[:, 0:64], glog, T_, [kl, tkn], [pk[0]])
            mm(P0[0:64, 64:192], R_, glog, [kl, rkn], [pk[0]])
            ke = ("gtmp", d)
            yield
            act(o.eb, P0[:, 0:64], AF.Exp, [pk[0]], [(ke, "eb")])
            act(o.enb, P0[:, 0:64], AF.Exp, [pk[0]], [(ke, "enb")], scale=-1.0)
            lastcol = 63 if d == 0 else 0
            act(o.gam, P0[:, lastcol:lastcol + 1], AF.Exp, [pk[0]], [(ke, "gam")])
            act(o.kte, P0[0:64, 64:192], AF.Exp, [pk[0]], [(ke, "kte")])
            yield
            tt("dve", o.kteb, o.kte, ktok, ALU.mult, [(ke, "kte"), kg], [(ke, "kteb")])
            stt("dve", o.qd, qT, GLA_SCALE, o.eb, ALU.mult, ALU.mult, [kq, (ke, "eb")], [(ke, "qd")])
            tt("dve", o.kd, kT, o.enb, ALU.mult, [kq, (ke, "enb")], [(ke, "kd")])
            tt("dve", o.qh.rearrange("p (h t) -> p h t", h=4), o.qd.unsqueeze(1).to_broadcast([128, 4, 64]),
               hm.unsqueeze(2).to_broadcast([128, 4, 64]), ALU.mult, [(ke, "qd"), "hm"], [(ke, "qh")])
            yield
            mm(P1[0:64, 0:256], o.kd, o.qh, [(ke, "kd"), (ke, "qh")], [pk[1]])
            mm(P1[:, 256:512], o.kteb, v, [(ke, "kteb"), kvk], [pk[1]])
            yield
            tt("dve", o.attm.rearrange("p (h t) -> p h t", h=4), P1[0:64, 0:256].rearrange("p (h t) -> p h t", h=4),
               T_.unsqueeze(1).to_broadcast([64, 4, 64]), ALU.mult, [pk[1], tkn], [(ke, "attm")])
            Sc = o.S[o.ns % 2]; Sk = ("S", d, o.ns % 2)
            Sn = o.S[(o.ns + 1) % 2]; Snk = ("S", d, (o.ns + 1) % 2)
            o.ns += 1
            mm(P2[0:64, 0:256], o.qd, Sc, [(ke, "qd"), Sk], [pk[2]], start=True, stop=True)
            tt("dve", o.tkv, P1[:, 256:512], bdm, ALU.mult, [pk[1], "bdm"], [(ke, "tkv")])
            stt("dve", Sn, Sc, o.gam, o.tkv, ALU.mult, ALU.add, [Sk, (ke, "gam"), (ke, "tkv")], [Snk])
            yield
            for h in range(4):
                mm(P3[0:64, 64 * h:64 * (h + 1)], o.attm[:, 64 * h:64 * (h + 1)], v[:, 64 * h:64 * (h + 1)], [(ke, "attm"), kvk], [pk[3]])
            okey = ("obuf", c)
            yield
            if first_visit:
                cp("act", obuf[:, c, :], P2[0:64, 0:256], [pk[2]], [okey])
            else:
                tt("dve", obuf[:, c, :], obuf[:, c, :], P2[0:64, 0:256], ALU.add, [okey, pk[2]], [okey])
            tt("dve", obuf[:, c, :], obuf[:, c, :], P3[0:64, 0:256], ALU.add, [okey, pk[3]], [okey])

        def epilogue(c):
            gi, c0, gn = grp_of(c)
            ci = c - c0
            okey = ("obuf", c)
            g_ = None
            ob = obuf[:, c, :]
            tt("dve", osq, ob, ob, ALU.mult, [okey], ["osq"])
            op("dve", lambda e: e.tensor_reduce(out=ss4, in_=osq.rearrange("p (h t) -> p h t", h=4), axis=AX.X, op=ALU.add),
               reads=["osq"], writes=["ss4"])
            yield
            act(ss4, ss4, AF.Sqrt, ["ss4", "eps64"], ["ss4"], bias=eps64, scale=1.0 / 64)
            recip(ss4, ss4, ["ss4"], ["ss4"])
            tt("dve", osum.rearrange("p (h t) -> p h t", h=4), ob.rearrange("p (h t) -> p h t", h=4),
               ss4.unsqueeze(2).to_broadcast([64, 4, 64]), ALU.mult, [okey, "ss4"], ["osum"])
            yield
            tt("pool", osum, osum, gngb, ALU.mult, ["osum", "gngb"], ["osum"])
            gt = gld[c % 2]; gk_ = ("gld", c % 2)
            dma("sp", gt, gtm_d[c * 64:(c + 1) * 64, 384:640], [], [gk_])
            act(sil, gt, AF.Silu, [gk_], ["sil"])
            tt("dve", osum, osum, sil, ALU.mult, ["osum", "sil"], ["osum"])
            yield
            if gi not in stgs:
                stgs[gi] = [0]
            stg = stg2[gi]; sk_ = ("gstg", gi)
            pe_ = PS[3]; pek = ("PS", 3)
            for hh in range(2):
                tr(pe_[:, 64 * hh:64 * (hh + 1)], osum[:, 128 * hh:128 * (hh + 1)], identf[0:64, 0:64], ["osum", "identf"], [pek])
            cp("act", stg[:, :, ci * 64:(ci + 1) * 64], pe_[:, 0:128].rearrange("p (h t) -> p h t", h=2), [pek], [sk_])
            stgs[gi][0] += 1
            if stgs[gi][0] == gn:
                dma("pool", mix_d[0:256, c0 * 64:(c0 + gn) * 64].rearrange("(h p) t -> p h t", p=128), stg[:, :, 0:gn * 64], [sk_], [])

        gld = [ar.f32(256, parts=64) for _ in range(2)]
        pstg = [ar.f32(1024) for _ in range(2)]; pcst = [ar.bf16(1024) for _ in range(2)]

        def precast_units():
            n = 0
            for wsrc, wdst in ((wg, wgb_d), (wu, wub_d)):
                for fc in range(NFC):
                    s_ = pstg[n % 2]; sk_ = ("pstg", n % 2); c_ = pcst[n % 2]; ck_ = ("pcst", n % 2)
                    dma("sp", s_.rearrange("p (k c) -> p k c", k=8), wsrc[l, :, fc * 128:(fc + 1) * 128].rearrange("(k p) c -> p k c", p=128), [], [sk_])
                    yield
                    cp(("act", "pool")[n % 2], c_, s_, [sk_], [ck_])
                    yield
                    dma("pool", wdst[fc], c_, [ck_], ["wgu_d"])
                    n += 1
                    yield

        pcg = precast_units()

        def pc_some(k_):
            for _ in range(k_):
                try:
                    next(pcg)
                except StopIteration:
                    return
                yield

        stg2 = [ar.bf16(2 * 512).rearrange("p (h t) -> p h t", h=2) for _ in range(1 + (NCH - NCC + 7) // 8)]
        fo = list(range(NCH))
        bo = list(range(NCC - 1, -1, -1)) + list(range(NCH - 1, NCC - 1, -1))
        seen = set()
        pend_epi = []

        def run_gens(gens):
            gens = list(gens)
            while gens:
                for g_ in list(gens):
                    try:
                        next(g_)
                    except StopIteration:
                        gens.remove(g_)

        for s_ in range(NCH):
            gens = []
            newdone = []
            for o, c in ((dirs[0], fo[s_]), (dirs[1], bo[s_])):
                gens.append(chunk(o, c, c not in seen))
                if c in seen:
                    newdone.append(c)
                seen.add(c)
            def epis(cs):
                for c_ in cs:
                    yield from epilogue(c_)
            if pend_epi:
                gens.append(epis(pend_epi))
            gens.append(pc_some(3))
            run_gens(gens)
            pend_epi = newdone
        if pend_epi:
            for c_ in pend_epi:
                for _ in epilogue(c_):
                    pass
        for _ in pc_some(10 ** 6):
            pass
        barrier()
    def phase_SWA(l):
        ar.reset()
        NB = TL // 128
        kT = ar.bf16(2 * T, parts=64).rearrange("p (g t) -> p g t", g=2)
        dma("sp", kT, sk_d.rearrange("(g d) t -> d g t", d=64), [], ["kT"])
        vaug = ar.bf16(NT128 * 256).rearrange("p (c g e) -> p c g e", c=NT128, g=2)
        memset("pool", vaug, 1.0, ["vaug"])
        for g_ in range(2):
            dma("sp", vaug[:, :, g_, 0:64], sv_d[:, 64 * g_:64 * (g_ + 1)].rearrange("(c p) d -> p c d", p=128), [], ["vaug"])
        esink = ar.f32(8, parts=64)
        dma("sp", esink, sink[l:l + 1, :].partition_broadcast(64), [], ["esink"])
        act(esink, esink, AF.Exp, ["esink"], ["esink"])
        mlo = ar.bf16(128); mhi = ar.bf16(128)
        tri_mask(mlo, "mlo", "ge", 128)
        tri_mask(mhi, "mhi", "le", 128)
        q8s = [ar.bf16(8 * 128, parts=64).rearrange("p (h t) -> p h t", h=8) for _ in range(2)]
        pts = [ar.bf16(512) for _ in range(4)]
        dens = [ar.f32(512, parts=64) for _ in range(2)]
        osb = [ar.bf16(512, parts=64) for _ in range(2)]
        blocks = []
        if l < L - 1:
            blocks += [(0, [(0, None), (1, None)]), (128, [(0, None), (1, None)])]
        for n in range(NB):
            kl = [(0, None), (1, None)]
            if n > 0: kl.append((2 + n - 1, "lo"))
            kl.append((2 + n, None))
            if n < NB - 1: kl.append((2 + n + 1, "hi"))
            blocks.append((CTX + 128 * n, kl))
        cnt = {"npt": 0, "nps": 0, "nd": 0}
        shwb = []
        for i_ in range(3):
            t_ = ar.f32(1152)
            dma("sp", t_, r_shw[l, i_:i_ + 1, :].partition_broadcast(128), [], [("shwb", i_)])
            shwb.append(t_)
        p0b = [[ar.f32(1152) for _ in range(4)] for _ in range(2)]
        NT0 = T // 128
        NCC0 = CTX // 128

        def p0(c):
            pm, pc, pn, sv_ = p0b[c % 2]
            kk_ = lambda nm: ("p0", c % 2, nm)
            t0 = 128 * c
            first = (c == 0 or c == NCC0)
            lastc = (c == NCC0 - 1 or c == NT0 - 1)
            dma("sp", pc, rw_d[t0:t0 + 128, :], [], [kk_("pc")])
            if first:
                memset("pool", pm, 0.0, [kk_("pm")])
                dma("sp", pm[1:128, :], rw_d[t0:t0 + 127, :], [], [kk_("pm")])
            else:
                dma("sp", pm, rw_d[t0 - 1:t0 + 127, :], [], [kk_("pm")])
            if lastc:
                memset("pool", pn, 0.0, [kk_("pn")])
                dma("sp", pn[0:127, :], rw_d[t0 + 1:t0 + 128, :], [], [kk_("pn")])
            else:
                dma("sp", pn, rw_d[t0 + 1:t0 + 129, :], [], [kk_("pn")])
            yield
            tt("pool", sv_, pm, shwb[0], ALU.mult, [kk_("pm"), ("shwb", 0)], [kk_("s")])
            tt("dve", pc, pc, shwb[1], ALU.mult, [kk_("pc"), ("shwb", 1)], [kk_("pc")])
            yield
            tt("pool", pn, pn, shwb[2], ALU.mult, [kk_("pn"), ("shwb", 2)], [kk_("pn")])
            tt("dve", sv_, sv_, pc, ALU.add, [kk_("pc"), kk_("s")], [kk_("s")])
            yield
            tt("pool", sv_, sv_, pn, ALU.add, [kk_("pn"), kk_("s")], [kk_("s")])
            dma("pool", rws_d[t0:t0 + 128, :], sv_, [kk_("s")], [])
            yield

        p0_next = [0]

        def p0_some(n):
            for _ in range(n):
                if p0_next[0] < NT0:
                    c_ = p0_next[0]; p0_next[0] += 1
                    yield from p0(c_)

        def block(tq0, kl, bi):
            q8 = q8s[bi % 2]; qk = ("q8", bi % 2)
            dma("sp", q8, sq_d[:, tq0:tq0 + 128].rearrange("(h d) t -> d h t", d=64), [], [qk])
            for g in range(2):
                po = PS[4 + 2 * (bi % 2) + g]; pok = ("PS", 4 + 2 * (bi % 2) + g)
                for ik, (c, msk) in enumerate(kl):
                    nps = cnt["nps"]; cnt["nps"] += 1; ps = PS[nps % 4]; psk = ("PS", nps % 4)
                    mm(ps, kT[:, g, c * 128:(c + 1) * 128], q8[:, 4 * g:4 * g + 4, :], ["kT", qk], [psk])
                    npt = cnt["npt"]; cnt["npt"] += 1; pt = pts[npt % 4]; ptk = ("pt", npt % 4)
                    act(pt, ps, AF.Exp, [psk], [ptk])
                    if msk is not None:
                        mt_ = mlo if msk == "lo" else mhi
                        tt("pool" if ik % 2 else "dve", pt.rearrange("p (h t) -> p h t", h=4), pt.rearrange("p (h t) -> p h t", h=4),
                           mt_.unsqueeze(1).to_broadcast([128, 4, 128]), ALU.mult, [ptk, "m" + msk], [ptk])
                    mm(po, vaug[:, c, g, :], pt, ["vaug", ptk], [pok], start=(ik == 0), stop=(ik == len(kl) - 1))
                    yield
                nd = cnt["nd"]; cnt["nd"] += 1
                den = dens[nd % 2]; dk = ("den", nd % 2)
                ob = osb[nd % 2]; obk = ("osb", nd % 2)
                yield
                cp("act", den, po[64:128, :], [pok], [dk])
                tt("dve", den.rearrange("p (h t) -> p h t", h=4), den.rearrange("p (h t) -> p h t", h=4),
                   esink[:, 4 * g:4 * g + 4].unsqueeze(2).to_broadcast([64, 4, 128]), ALU.add, [dk, "esink"], [dk])
                recip(den, den, [dk], [dk])
                tt("dve", ob, po[0:64, :], den, ALU.mult, [pok, dk], [obk])
                dma("pool", mix_d[256 + 256 * g:256 + 256 * (g + 1), tq0:tq0 + 128].rearrange("(h d) t -> d h t", d=64),
                    ob.rearrange("p (h t) -> p h t", h=4), [obk], [])
        def run_gens(gens):
            gens = list(gens)
            while gens:
                for g_ in list(gens):
                    try:
                        next(g_)
                    except StopIteration:
                        gens.remove(g_)

        for b0 in range(0, len(blocks), 2):
            run_gens([block(blocks[b0 + i][0], blocks[b0 + i][1], b0 + i) for i in range(2) if b0 + i < len(blocks)] + [p0_some(2)])
        run_gens([p0_some(NT0)])
        barrier()
    RC = 128
    rwA_d = dscr("rwA", [T // 64, 2, 64, 1024])
    rwE_d = dscr("rwE", [T // 64, 64, 512])
    CDEC = 0.6065306597126334

    def phase_RWKV(l):
        ar.reset()
        NCH = T // RC
        NCC = CTX // RC
        NCH64 = T // 64
        NCC64 = CTX // 64
        def h4(x):
            return x.rearrange("p (h d) -> p h d", h=4)
        mk = {}
        for kind in ("lt", "le", "gt", "ge"):
            t_ = ar.f32(128)
            tri_mask(t_, "r_" + kind, kind, 128)
            mk[kind] = t_
        BD = ar.f32(128)
        memset("pool", BD, 0.0, ["BD"])
        memset("pool", BD[0:64, 0:64], 1.0, ["BD"])
        memset("pool", BD[64:128, 64:128], 1.0, ["BD"])
        for kind in ("lt", "le", "gt", "ge"):
            tt("dve", mk[kind], mk[kind], BD, ALU.mult, ["r_" + kind, "BD"], ["r_" + kind])
        sel = ar.f32(2)
        memset("pool", sel, 0.0, ["sel"])
        memset("pool", sel[0:64, 0:1], 1.0, ["sel"])
        memset("pool", sel[64:128, 1:2], 1.0, ["sel"])
        BS = {0: "lt", 1: "gt"}; BI = {0: "le", 1: "ge"}; BST = {0: "gt", 1: "lt"}; REM = {0: "gt", 1: "lt"}
        MA = []; MB = []; MC = []
        for d in range(2):
            ma = ar.f32(256); mb = ar.f32(256); mc = ar.f32(128)
            ts("dve", ma[:, 0:128], mk[BS[d]], -1.0, None, ALU.mult, None, ["r_" + BS[d]], [("MA", d)])
            cp("dve", ma[:, 128:256], mk[BI[d]], ["r_" + BI[d]], [("MA", d)])
            cp("dve", mb[:, 0:128], mk[BS[d]], ["r_" + BS[d]], [("MB", d)])
            cp("dve", mb[:, 128:256], mk[BI[d]], ["r_" + BI[d]], [("MB", d)])
            ts("dve", mc, mk[BST[d]], -1.0, None, ALU.mult, None, ["r_" + BST[d]], [("MC", d)])
            MA.append(ma); MB.append(mb); MC.append(mc)
        id64 = identf[0:64, 0:64]
        def bc(src_ap, n, key):
            t_ = ar.f32(n)
            dma("sp", t_, src_ap.partition_broadcast(128), [], [key])
            return t_
        kkb = bc(r_kk[l:l + 1, :], 256, "kkb"); kab = bc(r_ka[l:l + 1, :], 256, "kab"); rkb = bc(r_rk[l:l + 1, :], 256, "rkb")
        lngb = bc(r_lng[l:l + 1, :], 256, "lngb"); lnbb = bc(r_lnb[l:l + 1, :], 256, "lnbb")
        w0b = ar.f32(512); a0b = ar.f32(512)
        for d in range(2):
            dma("sp", w0b[:, 256 * d:256 * (d + 1)], r_w0[l, d:d + 1, :].partition_broadcast(128), [], ["w0b"])
            dma("sp", a0b[:, 256 * d:256 * (d + 1)], r_a0[l, d:d + 1, :].partition_broadcast(128), [], ["a0b"])
        W2blk = ar.f32(512); A2blk = ar.f32(512); g2t = ar.f32(256)
        memset("pool", W2blk, 0.0, ["W2blk"]); memset("pool", A2blk, 0.0, ["A2blk"])
        for d in range(2):
            dma("sp", W2blk[64 * d:64 * (d + 1), 256 * d:256 * (d + 1)], r_w2[l, d], [], ["W2blk"])
            dma("sp", A2blk[64 * d:64 * (d + 1), 256 * d:256 * (d + 1)], r_a2[l, d], [], ["A2blk"])
        dma("sp", g2t, r_g2[l], [], ["g2t"])
        e12 = ar.f32(1); egn = ar.f32(1)
        memset("pool", e12, 1e-12, ["e12"]); memset("pool", egn, 64e-5, ["egn"])

        mark_slots = ar.off

        def mkslot(si):
            o = K()
            o.si = si
            o.s = ar.f32(1152)
            o.kk = ar.f32(256); o.t256 = ar.f32(256); o.s4 = ar.f32(4)
            o.E = ar.f32(512)
            o.sx = ar.f32(384)
            o.TT = ar.f32(384)
            o.sig = ar.f32(512); o.aa = ar.f32(512)
            o.kd = ar.f32(256); o.b = ar.f32(256)
            o.ei = ar.bf16(256); o.en = ar.bf16(256); o.ex = ar.bf16(256); o.er = ar.bf16(256)
            o.gam = ar.f32(8, parts=64)
            o.Rp = ar.bf16(256); o.KKp = ar.bf16(256); o.Bm = ar.bf16(256); o.KDm = ar.bf16(256)
            o.Bend = ar.bf16(256); o.KDend = ar.bf16(256); o.vb = ar.bf16(256)
            o.Bs = [ar.bf16(256), ar.bf16(256)]; o.KDs = [ar.bf16(256), ar.bf16(256)]
            o.KR = ar.bf16(1024, parts=64); o.BK = ar.bf16(1024, parts=64)
            o.SA = ar.bf16(1024); o.SB = ar.bf16(1024); o.Y0 = ar.bf16(512)
            o.Pc = [ar.bf16(512), ar.bf16(512)]
            o.YY = [ar.bf16(1024), ar.bf16(1024)]
            o.RHSK = ar.bf16(512); o.KU = ar.bf16(512); o.Dg = ar.f32(512, parts=64)
            o.OUTa = ar.f32(2 * 768, parts=64); o.OUTb = ar.f32(256)
            return o

        def K_(o, nm):
            return ("rs", o.si, nm)

        gnp = [0]

        def nb(o):
            i = gnp[0] % 8; gnp[0] += 1
            return PS[i], ("PS", i)

        def p1(c, o):
            t0 = RC * c
            first = (c == 0 or c == NCC)
            lastc = (c == NCC - 1 or c == NCH - 1)
            k_ = lambda nm: (("rsh", nm) if nm in ("pm", "pc", "pn") else K_(o, nm))
            dma("sp", o.s, rws_d[t0:t0 + RC, :], [], [k_("s")])
            yield
            S = o.s; sk_ = k_("s")
            r = S[:, 0:256]; k = S[:, 256:512]; v = S[:, 512:768]
            cp("act", o.vb, v, [sk_], [k_("vb")])
            vb = o.vb; vbk = k_("vb")
            tt("pool", o.kk, k, kkb, ALU.mult, [sk_, "kkb"], [k_("kk")])
            tt("dve", o.t256, o.kk, o.kk, ALU.mult, [k_("kk")], [k_("t256")])
            op("dve", lambda e: e.tensor_reduce(out=o.s4, in_=h4(o.t256), axis=AX.X, op=ALU.add), reads=[k_("t256")], writes=[k_("s4")])
            act(o.s4, o.s4, AF.Sqrt, [k_("s4"), "e12"], [k_("s4")], bias=e12)
            recip(o.s4, o.s4, [k_("s4")], [k_("s4")])
            tt("dve", h4(o.kk), h4(o.kk), o.s4.unsqueeze(2).to_broadcast([128, 4, 64]), ALU.mult, [k_("kk"), k_("s4")], [k_("kk")])
            yield
            tt("pool", o.t256, r, k, ALU.mult, [sk_, k_("t256")], [k_("t256")])
            tt("pool", o.t256, o.t256, rkb, ALU.mult, [k_("t256"), "rkb"], [k_("t256")])
            op("dve", lambda e: e.tensor_reduce(out=o.s4, in_=h4(o.t256), axis=AX.X, op=ALU.add), reads=[k_("t256")], writes=[k_("s4")])
            tt("dve", h4(o.E[:, 0:256]), h4(v), o.s4.unsqueeze(2).to_broadcast([128, 4, 64]), ALU.mult, [sk_, k_("s4")], [k_("E")])
            act(o.sx[:, 0:128], S[:, 1024:1152], AF.Sigmoid, [sk_], [k_("sx")])
            act(o.sx[:, 128:256], S[:, 768:896], AF.Tanh, [sk_], [k_("sx")])
            cp("pool", o.sx[:, 256:384], S[:, 896:1024], [sk_], [k_("sx")])
            pb, pbk = nb(o)
            for i in range(3):
                tr(pb[:, 128 * i:128 * (i + 1)], o.sx[:, 128 * i:128 * (i + 1)], identf, [k_("sx"), "identf"], [pbk])
            cp("act", o.TT, pb[:, 0:384], [pbk], [k_("TT")])
            yield
            pg, pgk = nb(o)
            mm(pg[:, 0:256], o.TT[:, 0:128], g2t, [k_("TT"), "g2t"], [pgk])
            cp("act", o.E[:, 256:512], pg[:, 0:256], [pgk], [k_("E")])
            dma("pool", rwE_d[2 * c:2 * c + 2].rearrange("s p n -> (s p) n"), o.E, [k_("E")], [])
            pz, pzk = nb(o)
            mm(pz, o.TT[:, 128:256], W2blk, [k_("TT"), "W2blk"], [pzk])
            tt("dve", o.sig, pz, w0b, ALU.add, [pzk, "w0b"], [k_("sig")])
            act(o.sig, o.sig, AF.Sigmoid, [k_("sig")], [k_("sig")])
            pa, pak = nb(o)
            mm(pa, o.TT[:, 256:384], A2blk, [k_("TT"), "A2blk"], [pak])
            tt("dve", o.aa, pa, a0b, ALU.add, [pak, "a0b"], [k_("aa")])
            act(o.aa, o.aa, AF.Sigmoid, [k_("aa")], [k_("aa")])
            yield
            for d in range(2):
                sg = o.sig[:, 256 * d:256 * (d + 1)]; a = o.aa[:, 256 * d:256 * (d + 1)]
                stt("dve", o.kd, a, -1.0, kab, ALU.add, ALU.mult, [k_("aa"), "kab"], [k_("kd")])
                stt("dve", o.kd, o.kd, 1.0, k, ALU.add, ALU.mult, [k_("kd"), sk_], [k_("kd")])
                tt("pool", o.b, a, o.kk, ALU.mult, [k_("aa"), k_("kk")], [k_("b")])
                pi, pik = nb(o)
                mm(pi[:, 0:256], mk[BI[d]], sg, ["r_" + BI[d], k_("sig")], [pik])
                mm(pi[:, 256:512], mk[BS[d]], sg, ["r_" + BS[d], k_("sig")], [pik])
                pr, prk = nb(o)
                mm(pr[:, 0:256], mk[REM[d]], sg, ["r_" + REM[d], k_("sig")], [prk])
                for h in range(4):
                    mm(pr[0:64, 256 + 2 * h:258 + 2 * h], sg[:, 64 * h:64 * (h + 1)], sel, [k_("sig"), "sel"], [prk])
                act(o.ei, pi[:, 0:256], AF.Exp, [pik], [k_("ei")], scale=-CDEC)
                act(o.en, pi[:, 0:256], AF.Exp, [pik], [k_("en")], scale=CDEC)
                act(o.ex, pi[:, 256:512], AF.Exp, [pik], [k_("ex")], scale=-CDEC)
                act(o.er, pr[:, 0:256], AF.Exp, [prk], [k_("er")], scale=-CDEC)
                act(o.gam, pr[0:64, 256:264], AF.Exp, [prk], [k_("gam")], scale=-CDEC)
                yield
                tt("dve", o.Rp, r, o.ei, ALU.mult, [sk_, k_("ei")], [k_("Rp")])
                tt("pool", o.KKp, o.kk, o.ex, ALU.mult, [k_("kk"), k_("ex")], [k_("KKp")])
                tt("dve", o.Bm, o.b, o.en, ALU.mult, [k_("b"), k_("en")], [k_("Bm")])
                tt("pool", o.KDm, o.kd, o.en, ALU.mult, [k_("kd"), k_("en")], [k_("KDm")])
                tt("dve", o.Bend, o.b, o.er, ALU.mult, [k_("b"), k_("er")], [k_("Bend")])
                tt("pool", o.KDend, o.kd, o.er, ALU.mult, [k_("kd"), k_("er")], [k_("KDend")])
                for s_ in range(2):
                    ts("dve", o.Bs[s_], o.Bend, sel[:, s_:s_ + 1], None, ALU.mult, None, [k_("Bend"), "sel"], [k_("Bs")])
                    ts("dve", o.KDs[s_], o.KDend, sel[:, s_:s_ + 1], None, ALU.mult, None, [k_("KDend"), "sel"], [k_("KDs")])
                px_, pxk = nb(o)
                py_, pyk = nb(o)
                px = px_.bitcast(BF16); py = py_.bitcast(BF16)
                for h in range(4):
                    hs = slice(64 * h, 64 * (h + 1))
                    tr(px[0:64, 256 * h:256 * h + 128], o.KKp[:, hs], identb, [k_("KKp"), "identb"], [pxk])
                    tr(px[0:64, 256 * h + 128:256 * h + 256], o.Rp[:, hs], identb, [k_("Rp"), "identb"], [pxk])
                    tr(py[0:64, 128 * h:128 * h + 128], o.Bm[:, hs], identb, [k_("Bm"), "identb"], [pyk])
                    tr(py[0:64, 512 + 128 * h:512 + 128 * h + 128], o.KDm[:, hs], identb, [k_("KDm"), "identb"], [pyk])
                cp("act", o.KR, px[0:64, :], [pxk], [k_("KR")])
                cp("dve", o.BK, py[0:64, :], [pyk], [k_("BK")])
                yield
                KR3 = o.KR.rearrange("p (h n) -> p h n", h=4)
                pA = []; pB = []
                for i in range(2):
                    pA.append(nb(o)); pB.append(nb(o))
                pC, pCk = nb(o)
                for h in range(4):
                    BmT = o.BK[:, 128 * h:128 * (h + 1)]; KDmT = o.BK[:, 512 + 128 * h:512 + 128 * (h + 1)]
                    mm(pA[h // 2][0][:, 256 * (h % 2):256 * (h % 2 + 1)], BmT, KR3[:, h, :], [k_("BK"), k_("KR")], [pA[h // 2][1]])
                    mm(pB[h // 2][0][:, 256 * (h % 2):256 * (h % 2 + 1)], KDmT, KR3[:, h, :], [k_("BK"), k_("KR")], [pB[h // 2][1]])
                    mm(pC[:, 128 * h:128 * (h + 1)], KR3[:, h, 0:128], BmT, [k_("BK"), k_("KR")], [pCk])
                SA3 = o.SA.rearrange("p (h n) -> p h n", h=4); SB3 = o.SB.rearrange("p (h n) -> p h n", h=4)
                Y03 = o.Y0.rearrange("p (h n) -> p h n", h=4)
                for i in range(2):
                    tt("dve", SA3[:, 2 * i:2 * i + 2, :], pA[i][0].rearrange("p (h n) -> p h n", h=2), MA[d].unsqueeze(1).to_broadcast([128, 2, 256]),
                       ALU.mult, [pA[i][1], ("MA", d)], [k_("SA")])
                    tt("dve", SB3[:, 2 * i:2 * i + 2, :], pB[i][0].rearrange("p (h n) -> p h n", h=2), MB[d].unsqueeze(1).to_broadcast([128, 2, 256]),
                       ALU.mult, [pB[i][1], ("MB", d)], [k_("SB")])
                tt("dve", Y03, pC.rearrange("p (h n) -> p h n", h=4), MC[d].unsqueeze(1).to_broadcast([128, 4, 128]), ALU.mult, [pCk, ("MC", d)], [k_("Y0")])
                yield
                YP = [o.YY[0].rearrange("p (h n) -> p h n", h=4), o.YY[1].rearrange("p (h n) -> p h n", h=4)]
                Yb = [o.Pc[0].rearrange("p (h n) -> p h n", h=4), o.Pc[1].rearrange("p (h n) -> p h n", h=4)]
                ypk = [k_("YP0"), k_("YP1")]; ybk = [k_("Yb0"), k_("Yb1")]
                pq = [nb(o), nb(o)]
                for h in range(4):
                    pq_, pqk_ = pq[h // 2]
                    mm(pq_[:, 256 * (h % 2):256 * (h % 2) + 128], Y03[:, h, :], SA3[:, h, 0:128], [k_("SA"), k_("Y0")], [pqk_])
                    mm(pq_[:, 256 * (h % 2) + 128:256 * (h % 2) + 256], SA3[:, h, 0:128], Y03[:, h, :], [k_("SA"), k_("Y0")], [pqk_])
                for j in range(2):
                    src_ = pq[j][0].rearrange("p (h n) -> p h n", h=2)
                    cp("act", YP[1][:, 2 * j:2 * j + 2, 0:128], src_[:, :, 0:128], [pq[j][1]], [ypk[1]])
                    cp("act", Yb[1][:, 2 * j:2 * j + 2, :], src_[:, :, 128:256], [pq[j][1]], [ybk[1]])
                tt("pool", YP[1][:, :, 128:256], SA3[:, :, 0:128], identf.unsqueeze(1).to_broadcast([128, 4, 128]), ALU.add, [k_("SA"), "identf"], [ypk[1]])
                yield
                cur = 1
                for i in range(1, 6):
                    nxt = 1 - cur
                    last = (i == 5)
                    pq = [nb(o), nb(o)]
                    pn2, pn2k = nb(o)
                    for h in range(4):
                        pq_, pqk_ = pq[h // 2]
                        if not last:
                            mm(pq_[:, 256 * (h % 2):256 * (h % 2) + 256], Yb[cur][:, h, :], YP[cur][:, h, :], [ybk[cur], ypk[cur]], [pqk_])
                            mm(pn2[:, 128 * h:128 * (h + 1)], YP[cur][:, h, 0:128], Yb[cur][:, h, :], [ybk[cur], ypk[cur]], [pn2k])
                        else:
                            mm(pq_[:, 256 * (h % 2) + 128:256 * (h % 2) + 256], Yb[cur][:, h, :], YP[cur][:, h, 128:256], [ybk[cur], ypk[cur]], [pqk_])
                    for j in range(2):
                        src_ = pq[j][0].rearrange("p (h n) -> p h n", h=2)
                        if not last:
                            cp("act", YP[nxt][:, 2 * j:2 * j + 2, 0:128], src_[:, :, 0:128], [pq[j][1]], [ypk[nxt]])
                        tt("dve", YP[nxt][:, 2 * j:2 * j + 2, 128:256], YP[cur][:, 2 * j:2 * j + 2, 128:256], src_[:, :, 128:256], ALU.add,
                           [ypk[cur], pq[j][1]], [ypk[nxt]])
                    if not last:
                        cp("act", Yb[nxt], pn2.rearrange("p (h n) -> p h n", h=4), [pn2k], [ybk[nxt]])
                    cur = nxt
                    yield
                Pfin = YP[cur]; Pk = ypk[cur]
                RH3 = o.RHSK.rearrange("p (h n) -> p h n", h=4)
                pw, pwk = nb(o)
                for h in range(4):
                    mm(pw[:, 64 * h:64 * (h + 1)], SB3[:, h, 0:128], vb[:, 64 * h:64 * (h + 1)], [k_("SB"), vbk], [pwk])
                ts("dve", RH3[:, :, 64:128], h4(pw[:, 0:256]), -1.0, None, ALU.mult, None, [pwk], [k_("RHSK")])
                cp("pool", RH3[:, :, 0:64], h4(o.KKp), [k_("KKp")], [k_("RHSK")])
                pu, puk = nb(o)
                for h in range(4):
                    mm(pu[:, 128 * h:128 * (h + 1)], Pfin[:, h, 128:256], RH3[:, h, :], [Pk, k_("RHSK")], [puk])
                cp("act", o.KU, pu, [puk], [k_("KU")])
                KU3 = o.KU.rearrange("p (h n) -> p h n", h=4)
                yield
                OA = o.OUTa.rearrange("p (s n) -> p s n", s=2)
                gam3 = o.gam.rearrange("p (h s) -> p h s", s=2)
                pg2, pg2k = nb(o)
                ph, phk = nb(o)
                for s_ in range(2):
                    for h in range(4):
                        hs = slice(64 * h, 64 * (h + 1))
                        cs = slice(256 * s_ + 64 * h, 256 * s_ + 64 * (h + 1))
                        mm(pg2[0:64, cs], KU3[:, h, 0:64], o.Bs[s_][:, hs], [k_("KU"), k_("Bs")], [pg2k])
                        mm(ph[0:64, cs], o.Bs[s_][:, hs], KU3[:, h, 64:128], [k_("KU"), k_("Bs")], [phk], start=True, stop=False)
                        mm(ph[0:64, cs], o.KDs[s_][:, hs], vb[:, hs], [k_("KDs"), vbk], [phk], start=False, stop=True)
                Dg3 = o.Dg.rearrange("p (s n) -> p s n", s=2)
                for s_ in range(2):
                    tt("pool", h4(Dg3[:, s_, :]), id64.unsqueeze(1).to_broadcast([64, 4, 64]), gam3[:, :, s_].unsqueeze(2).to_broadcast([64, 4, 64]),
                       ALU.mult, ["identf", k_("gam")], [k_("Dg")])
                    tt("dve", OA[:, s_, 0:256], Dg3[:, s_, :], pg2[0:64, 256 * s_:256 * (s_ + 1)], ALU.subtract, [k_("Dg"), pg2k], [k_("OUTa")])
                    cp("act", OA[:, s_, 256:512], ph[0:64, 256 * s_:256 * (s_ + 1)], [phk], [k_("OUTa")])
                prt, prtk = nb(o)
                for h in range(4):
                    mm(prt[0:64, 128 * h:128 * (h + 1)], KU3[:, h, 0:64], SA3[:, h, 128:256], [k_("KU"), k_("SA")], [prtk])
                prt3 = prt[0:64, :].rearrange("p (h n) -> p h n", h=4)
                for s_ in range(2):
                    tt("dve", OA[:, s_, 512:768].rearrange("p (h n) -> p h n", h=4), KR3[:, :, 128 + 64 * s_:128 + 64 * (s_ + 1)],
                       prt3[:, :, 64 * s_:64 * (s_ + 1)], ALU.subtract, [k_("KR"), prtk], [k_("OUTa")])
                po, pok = nb(o)
                for h in range(4):
                    hs = slice(64 * h, 64 * (h + 1))
                    mm(po[:, hs], SA3[:, h, 128:256], KU3[:, h, 64:128], [k_("KU"), k_("SA")], [pok], start=True, stop=False)
                    mm(po[:, hs], SB3[:, h, 128:256], vb[:, hs], [k_("SB"), vbk], [pok], start=False, stop=True)
                cp("act", o.OUTb, po[:, 0:256], [pok], [k_("OUTb")])
                for s_ in range(2):
                    dma("pool", rwA_d[2 * c + s_, d, :, 0:768], OA[:, s_, :], [k_("OUTa")], [])
                    dma("pool", rwA_d[2 * c + s_, d, :, 768:1024], o.OUTb[64 * s_:64 * (s_ + 1), :], [k_("OUTb")], [])
                yield

        def run_gens(gens):
            gens = list(gens)
            while gens:
                for g_ in list(gens):
                    try:
                        next(g_)
                    except StopIteration:
                        gens.remove(g_)

        NSL = 3
        slots = [mkslot(i) for i in range(NSL)]
        for c in range(0, NCH, NSL):
            run_gens([p1(c + i, slots[i]) for i in range(NSL) if c + i < NCH])
        barrier()
        import os
        if os.environ.get("RSTOP") == "1":
            return
        ar.off = mark_slots
        NCH = NCH64; NCC = NCC64
        lngb64 = lngb[0:64, :]; lnbb64 = lnbb[0:64, :]
        obuf = ar.f32(NCH * 256, parts=64).rearrange("p (c n) -> p c n", c=NCH)
        ds = []
        for d in range(2):
            o = K(); o.d = d
            o.A = [ar.f32(1024, parts=64) for _ in range(2)]
            o.Z = [ar.f32(256, parts=64) for _ in range(2)]
            o.n = 0
            memset("pool", o.Z[0], 0.0, [("Z", d, 0)])
            ds.append(o)
        Eb = [ar.f32(512, parts=64) for _ in range(4)]
        NP3 = 4
        p3b = [(ar.f32(256, parts=64), ar.f32(256, parts=64), ar.f32(4, parts=64), ar.f32(4, parts=64)) for _ in range(NP3)]
        NG = 1 + (NCH - NCC + 7) // 8
        stgs = [ar.bf16(2 * 512).rearrange("p (h t) -> p h t", h=2) for _ in range(NG)]
        scnt = [0] * NG

        def grp_of(c):
            return (0, 0, NCC) if c < NCC else (1 + (c - NCC) // 8, NCC + ((c - NCC) // 8) * 8, 8)

        def p2(o, c, first_visit):
            d = o.d
            A = o.A[o.n % 2]; Ak = ("A", d, o.n % 2)
            Zc = o.Z[o.n % 2]; Zk = ("Z", d, o.n % 2)
            Zn = o.Z[(o.n + 1) % 2]; Znk = ("Z", d, (o.n + 1) % 2)
            o.n += 1
            dma("sp", A, rwA_d[c, d], [], [Ak])
            A4 = A.rearrange("p (m n) -> p m n", m=4)
            pO = PS[4 * d]; pOk = ("PS", 4 * d); pZ = PS[4 * d + 1 + (o.n % 2)]; pZk = ("PS", 4 * d + 1 + (o.n % 2))
            for h in range(4):
                hs = slice(64 * h, 64 * (h + 1))
                mm(pZ[0:64, hs], A4[:, 0, hs], Zc[:, hs], [Ak, Zk], [pZk])
            tt("dve", Zn, pZ[0:64, 0:256], A4[:, 1, :], ALU.add, [pZk, Ak], [Znk])
            for h in range(4):
                hs = slice(64 * h, 64 * (h + 1))
                mm(pO[0:64, hs], A4[:, 2, hs], Zc[:, hs], [Ak, Zk], [pOk])
            okey = ("obuf", c)
            if first_visit:
                tt("pool" if False else "dve", obuf[:, c, :], pO[0:64, 0:256], A4[:, 3, :], ALU.add, [pOk, Ak], [okey])
            else:
                tt("dve", A4[:, 3, :], pO[0:64, 0:256], A4[:, 3, :], ALU.add, [pOk, Ak], [Ak])
                tt("pool", obuf[:, c, :], obuf[:, c, :], A4[:, 3, :], ALU.add, [okey, Ak], [okey])

        def p3(c):
            gi, c0, gn = grp_of(c)
            ci = c - c0
            okey = ("obuf", c)
            ob = obuf[:, c, :]
            cen, sq, m4, v4 = p3b[c % NP3]
            kc_ = lambda nm: (nm, c % NP3)
            E = Eb[c % NP3]; Ek = ("Eb", c % NP3)
            dma("sp", E, rwE_d[c], [], [Ek])
            op("dve", lambda e: e.tensor_reduce(out=m4, in_=h4(ob), axis=AX.X, op=ALU.add), reads=[okey], writes=[kc_("m4")])
            yield
            ts("dve", m4, m4, 1.0 / 64, None, ALU.mult, None, [kc_("m4")], [kc_("m4")])
            yield
            tt("dve", h4(cen), h4(ob), m4.unsqueeze(2).to_broadcast([64, 4, 64]), ALU.subtract, [okey, kc_("m4")], [kc_("cen")])
            yield
            tt("pool", sq, cen, cen, ALU.mult, [kc_("cen")], [kc_("sq")])
            yield
            op("dve", lambda e: e.tensor_reduce(out=v4, in_=h4(sq), axis=AX.X, op=ALU.add), reads=[kc_("sq")], writes=[kc_("v4")])
            yield
            act(v4, v4, AF.Sqrt, [kc_("v4"), "egn"], [kc_("v4")], bias=egn[0:64, :], scale=1.0 / 64)
            yield
            recip(v4, v4, [kc_("v4")], [kc_("v4")])
            yield
            tt("dve", h4(cen), h4(cen), v4.unsqueeze(2).to_broadcast([64, 4, 64]), ALU.mult, [kc_("cen"), kc_("v4")], [kc_("cen")])
            yield
            tt("pool", cen, cen, lngb64, ALU.mult, [kc_("cen"), "lngb"], [kc_("cen")])
            yield
            tt("pool", cen, cen, lnbb64, ALU.add, [kc_("cen"), "lnbb"], [kc_("cen")])
            yield
            tt("dve", cen, cen, E[:, 0:256], ALU.add, [kc_("cen"), Ek], [kc_("cen")])
            yield
            tt("dve", cen, cen, E[:, 256:512], ALU.mult, [kc_("cen"), Ek], [kc_("cen")])
            yield
            stg = stgs[gi]; sk_ = ("rstg", gi)
            pe_ = PS[3 + 4 * (c % 2)]; pek = ("PS", 3 + 4 * (c % 2))
            for hh in range(2):
                tr(pe_[:, 64 * hh:64 * (hh + 1)], cen[:, 128 * hh:128 * (hh + 1)], id64, [kc_("cen"), "identf"], [pek])
            cp("act", stg[:, :, ci * 64:(ci + 1) * 64], pe_[:, 0:128].rearrange("p (h t) -> p h t", h=2), [pek], [sk_])
            yield
            scnt[gi] += 1
            if scnt[gi] == gn:
                dma("pool", mix_d[768:1024, c0 * 64:(c0 + gn) * 64].rearrange("(h p) t -> p h t", p=128), stg[:, :, 0:gn * 64], [sk_], [])

        fo = list(range(NCH))
        bo = list(range(NCC - 1, -1, -1)) + list(range(NCH - 1, NCC - 1, -1))
        seen = set()
        for s_ in range(NCH):
            for o, c in ((ds[0], fo[s_]), (ds[1], bo[s_])):
                p2(o, c, c not in seen)
                seen.add(c)
        for c0_ in range(0, NCH, NP3):
            run_gens([p3(c0_ + i) for i in range(NP3) if c0_ + i < NCH])
        barrier()
    def phase_FFN(l):
        ar.reset()
        last = (l == L - 1)
        src = xin if l == 0 else xs
        wdb = ar.bf16(NFC * D).rearrange("p (f c) -> p f c", f=NFC)
        wob = ar.bf16(8 * D).rearrange("p (k c) -> p k c", k=8)
        g1b = [ar.f32(D) for _ in range(2)]
        g2b = [ar.f32(D) for _ in range(2)]
        for ti in range(2):
            dma("sp", g1b[ti], gates[l, ti:ti + 1, 0:1024].partition_broadcast(128), [], [("g1b", ti)])
            dma("sp", g2b[ti], gates[l, ti:ti + 1, 1024:2048].partition_broadcast(128), [], [("g2b", ti)])
        if last:
            fngb = ar.f32(D)
            dma("sp", fngb, fng.rearrange("(o d) -> o d", o=1).partition_broadcast(128), [], ["fngb"])
        mark = ar.off
        stg = [ar.f32(2048) for _ in range(2)]
        cstg = [ar.bf16(2048) for _ in range(2)]
        ns = 0
        engs3 = ("dve", "act", "pool")
        for kc in range(8):
            s_ = stg[ns % 2]; sk_ = ("stg", ns % 2)
            dma(("sp", "pool")[ns % 2], s_[:, 0:D], w_out[l, kc * 128:(kc + 1) * 128, :], [], [sk_])
            cp(engs3[ns % 3], wob[:, kc, :], s_[:, 0:D], [sk_], [("wob", kc)])
            ns += 1
        for f0 in range(0, NFC, 2):
            s_ = stg[ns % 2]; sk_ = ("stg", ns % 2)
            dma(("sp", "pool")[ns % 2], s_.rearrange("p (f c) -> p f c", f=2), wd[l, f0 * 128:(f0 + 2) * 128, :].rearrange("(f p) c -> p f c", p=128), [], [sk_])
            cp(engs3[ns % 3], wdb[:, f0:f0 + 2, :], s_.rearrange("p (f c) -> p f c", f=2), [sk_], [("wdb", f0), ("wdb", f0 + 1)])
            ns += 1
        for wsrc, wdst in ():
            for f0 in range(0, NFC, 2):
                s_ = stg[ns % 2]; sk_ = ("stg", ns % 2)
                c_ = cstg[ns % 2]; ck_ = ("cstg", ns % 2)
                dma("sp", s_.rearrange("p (k c) -> p k c", k=8), wsrc[l, :, f0 * 128:(f0 + 2) * 128].rearrange("(k p) c -> p k c", p=128), [], [sk_])
                cp(engs3[ns % 3], c_.rearrange("p (f k c) -> p k f c", f=2, k=8), s_.rearrange("p (k f c) -> p k f c", k=8, f=2), [sk_], [ck_])
                dma("pool", wdst[f0:f0 + 2].rearrange("f p c -> p f c"), c_.rearrange("p (f c) -> p f c", f=2), [ck_], ["wgu_d"])
                ns += 1
        barrier()
        ar.off = mark
        wobk = [("wob", kc) for kc in range(8)]
        h2T = ar.bf16(8 * 512).rearrange("p (k c) -> p k c", k=8)
        actT = ar.bf16(NFC * 512).rearrange("p (f c) -> p f c", f=NFC)
        x1 = [ar.f32(D) for _ in range(4)]
        xts = [ar.f32(D) for _ in range(2)]
        mts = [ar.bf16(8 * 128).rearrange("p (k c) -> p k c", k=8) for _ in range(2)]
        junk = ar.bf16(D); ss = ar.f32(1); rstd = ar.f32(1); xn = ar.bf16(D)
        PT = PS[7].bitcast(BF16)
        scr = (junk, ss, rstd, xn, PT)
        tmpf = [ar.f32(512) for _ in range(2)]
        sgs = [ar.f32(512) for _ in range(2)]
        NR = 3
        wgt = [ar.bf16(1024).rearrange("p (k c) -> p k c", k=8) for _ in range(NR)]
        wut = [ar.bf16(1024).rearrange("p (k c) -> p k c", k=8) for _ in range(NR)]
        nx = 0; ntmp = 0; nring = 0; nsg = 0; nyo = 0
        for (t0, NT, ti) in tiles():
            if last and ti == 1:
                continue
            nsub = NT // 128
            for j in range(nsub):
                tok = slice(t0 + 128 * j, t0 + 128 * (j + 1))
                xt = xts[nx % 2]; xk = ("xt", nx % 2)
                mt = mts[nx % 2]; mk = ("mt", nx % 2); nx += 1
                dma("sp", xt, src[tok, :], [], [xk])
                dma("sp", mt, mix_d[:, tok].rearrange("(k p) t -> p k t", p=128), ["mix_d"], [mk])
                for h in range(2):
                    ps = PS[h]; pk = ("PS", h)
                    for kc in range(8):
                        mm(ps, mt[:, kc, :], wob[:, kc, h * 512:(h + 1) * 512], [mk, wobk[kc]], [pk], start=(kc == 0), stop=(kc == 7))
                    tf = tmpf[ntmp % 2]; tk = ("tmpf", ntmp % 2); ntmp += 1
                    tt("dve", tf, ps, g1b[ti][:, h * 512:(h + 1) * 512], ALU.mult, [pk, ("g1b", ti)], [tk])
                    tt("pool", x1[j][:, h * 512:(h + 1) * 512], xt[:, h * 512:(h + 1) * 512], tf, ALU.add, [xk, tk], [("x1", j)])
                norm_mod_T(x1[j], ("x1", j), A2, B2, l, ti, h2T, ("h2T", j), 128 * j, scr)
            hk = [("h2T", j) for j in range(nsub)]
            for fc in range(NFC):
                r_ = nring % NR; nring += 1
                gk_ = ("wgt", r_); uk_ = ("wut", r_)
                dma("sp", wgt[r_], wgb_d[fc].rearrange("p (k c) -> p k c", k=8), ["wgu_d"], [gk_])
                dma("sp", wut[r_], wub_d[fc].rearrange("p (k c) -> p k c", k=8), ["wgu_d"], [uk_])
                pg = PS[2 + fc % 2]; pgk = ("PS", 2 + fc % 2)
                pu = PS[4 + fc % 2]; puk = ("PS", 4 + fc % 2)
                for kc in range(8):
                    mm(pg[:, :NT], wgt[r_][:, kc, :], h2T[:, kc, :NT], [gk_] + hk, [pgk], start=(kc == 0), stop=(kc == 7))
                for kc in range(8):
                    mm(pu[:, :NT], wut[r_][:, kc, :], h2T[:, kc, :NT], [uk_] + hk, [puk], start=(kc == 0), stop=(kc == 7))
                sg = sgs[nsg % 2]; sgk = ("sg", nsg % 2); nsg += 1
                act(sg[:, :NT], pg[:, :NT], AF.Silu, [pgk], [sgk])
                tt("dve", actT[:, fc, :NT], sg[:, :NT], pu[:, :NT], ALU.mult, [sgk, puk], [("actT", fc)])
            ak = [("actT", fc) for fc in range(NFC)]
            for j in range(nsub):
                tok = slice(t0 + 128 * j, t0 + 128 * (j + 1))
                y = x1[j]; yk = ("x1", j)
                for h in range(2):
                    ps = PS[h]; pk = ("PS", h)
                    for fc in range(NFC):
                        mm(ps, actT[:, fc, 128 * j:128 * (j + 1)], wdb[:, fc, h * 512:(h + 1) * 512], [ak[fc], ("wdb", fc)], [pk],
                           start=(fc == 0), stop=(fc == NFC - 1))
                    tf = tmpf[ntmp % 2]; tk = ("tmpf", ntmp % 2); ntmp += 1
                    tt("dve", tf, ps, g2b[ti][:, h * 512:(h + 1) * 512], ALU.mult, [pk, ("g2b", ti)], [tk])
                    tt("pool", y[:, h * 512:(h + 1) * 512], x1[j][:, h * 512:(h + 1) * 512], tf, ALU.add, [("x1", j), tk], [yk])
                if not last:
                    dma("pool", xs[tok, :], y, [yk], [])
                else:
                    act(junk, y, AF.Square, [yk], ["junk", "ss"], accum=ss)
                    act(rstd, ss, AF.Sqrt, ["ss", "epsc"], ["rstd"], bias=epsc, scale=1.0 / D)
                    recip(rstd, rstd, ["rstd"], ["rstd"])
                    stt("dve", y, y, rstd, fngb, ALU.mult, ALU.mult, [yk, "rstd", "fngb"], [yk])
                    dma("pool", out[t0 - CTX + 128 * j:t0 - CTX + 128 * (j + 1), :], y, [yk], [])
        barrier()
    phase_ada()
    for l in range(L):
        if "P" in phases:
            phase_P(l)
        if "GLA" in phases:
            phase_GLA(l)
        if "SWA" in phases:
            phase_SWA(l)
        if "RWKV" in phases:
            phase_RWKV(l)
        if "FFN" in phases:
            phase_FFN(l)
    fw.emit()
    es.close()
    return nc


def host_inputs(inp, b, TL, L):
    f = lambda a: np.ascontiguousarray(np.asarray(a, dtype=np.float32))
    m = {}
    m["xin"] = f(np.concatenate([inp["ctx"][b], inp["x"][b][:TL]], axis=0))
    cvec = np.stack([np.asarray(inp["c"][b]), np.asarray(inp["c_ctx"])], axis=-1)
    m["cc"] = f(cvec.reshape(8, 128, 2).transpose(1, 0, 2))
    return m


def shared_inputs(inp, TL, L):
    f = lambda a: np.ascontiguousarray(np.asarray(a, dtype=np.float32))
    m = {}
    for nm in ("ada_w", "ada_b", "w_out", "gla_decay_w", "gla_decay_b", "gla_norm_g", "swa_sink", "rwkv_shift_w",
               "rwkv_w0", "rwkv_w2", "rwkv_a0", "rwkv_a2", "rwkv_g2", "rwkv_k_k", "rwkv_k_a", "rwkv_ln_g", "rwkv_ln_b",
               "ffn_w_gate", "ffn_w_up", "ffn_w_down"):
        m[nm] = f(np.asarray(inp[nm])[:L])
    m["rwkv_r_k"] = f(np.asarray(inp["rwkv_r_k"])[:L].reshape(L, 256))
    m["final_norm_g"] = f(inp["final_norm_g"])
    m["nmg"] = f(np.asarray(inp["norm_mix_g"])[:L].reshape(L, 8, 128).transpose(0, 2, 1))
    m["nfg"] = f(np.asarray(inp["norm_ffn_g"])[:L].reshape(L, 8, 128).transpose(0, 2, 1))
    m["wr"] = relayout_w_in(f(np.asarray(inp["w_in"])[:L]))
    c_, s_ = rope_tables(TL)
    m["cosT"] = c_; m["sinT"] = s_
    return m


_CACHE = {}


def kernel(**inputs):
    TL, L, NCORE = 4096, 4, 8
    if "nc" not in _CACHE:
        _CACHE["nc"] = build(TL, L)
    nc = _CACHE["nc"]
    sh = shared_inputs(inputs, TL, L)
    in_maps = []
    for b in range(NCORE):
        m = dict(sh)
        m.update(host_inputs(inputs, b, TL, L))
        in_maps.append(m)
    res = run_bass_kernel_spmd(nc, in_maps, core_ids=list(range(NCORE)))
    return np.stack([np.asarray(r["out"], dtype=np.float32) for r in res.results], axis=0)
```
